# Optimizing a Trainium2 kernel written in Bass

```python
import math, functools
import jax, jax.numpy as jnp
from jax import lax
import numpy as np

D_MODEL = 1024
BATCH = 1
SEQ = 16384
DEPTH = 4

N_MIXERS = 3
CHUNK = 128
NORM_EPS = 1e-6
OUT_SCALE = 0.5

SSD_EXPAND = 2
SSD_DI = SSD_EXPAND * D_MODEL
SSD_HEADDIM = 64
SSD_HEADS = SSD_DI // SSD_HEADDIM
SSD_GROUPS = 8
SSD_STATE = 128
SSD_CONV = 4
SSD_CONV_DIM = SSD_DI + 2 * SSD_GROUPS * SSD_STATE
SSD_IN = SSD_DI + SSD_CONV_DIM + SSD_HEADS
DT_MIN = 0.001
DT_MAX = 0.1

RET_HEADS = 4
RET_DK = D_MODEL // RET_HEADS
RET_DV = 2 * RET_DK
RET_DVTOT = RET_HEADS * RET_DV
RET_THETA = 10000.0
RET_IN = 2 * RET_HEADS * RET_DK + 2 * RET_DVTOT

DSA_HEADS = 16
DSA_HEADDIM = 64
DSA_KV_HEADS = 4
DSA_GQA = DSA_HEADS // DSA_KV_HEADS
DSA_WIDTH = DSA_HEADS * DSA_HEADDIM
IDX_HEADS = 16
IDX_DIM = 64
TOPK_MAX = 256
Q_BLOCK = 128
ROPE_THETA = 500000.0
ROPE_FRACTION = 4
DSA_SPLITS = (DSA_WIDTH, DSA_KV_HEADS * DSA_HEADDIM, DSA_KV_HEADS * DSA_HEADDIM,
              DSA_WIDTH, IDX_HEADS * IDX_DIM, IDX_DIM, IDX_HEADS)
DSA_IN = sum(DSA_SPLITS)

kernel_name = "hybrid_ssd_retention_dsa_trunk"


def _rmsnorm(x, g):
    xf = x.astype(jnp.float32)
    y = xf * lax.rsqrt(jnp.mean(xf * xf, axis=-1, keepdims=True) + NORM_EPS)
    return (y * g.astype(jnp.float32)).astype(x.dtype)


def _rope(x, pos, inv_freq):
    half = inv_freq.shape[0]
    r = 2 * half
    ang = pos.astype(jnp.float32)[..., None] * inv_freq
    cos = jnp.cos(ang)[:, :, None, :]
    sin = jnp.sin(ang)[:, :, None, :]
    x1 = x[..., :half].astype(jnp.float32)
    x2 = x[..., half:r].astype(jnp.float32)
    rot = jnp.concatenate([x1 * cos - x2 * sin, x2 * cos + x1 * sin], axis=-1).astype(x.dtype)
    return jnp.concatenate([rot, x[..., r:]], axis=-1)


def _partial_inv_freq(head_dim):
    r = head_dim // ROPE_FRACTION
    return jnp.float32(ROPE_THETA) ** (-jnp.arange(0, r, 2, dtype=jnp.float32) / r)


def _split(t, sizes):
    idx = list(np.cumsum(sizes)[:-1])
    return jnp.split(t, idx, axis=-1)


def _causal_dwconv(x, w, b):
    c = x.shape[-1]
    y = lax.conv_general_dilated(x, w[:, None, :].astype(x.dtype), window_strides=(1,),
                                 padding=[(SSD_CONV - 1, 0)],
                                 dimension_numbers=("NWC", "WIO", "NWC"),
                                 feature_group_count=c)
    return y + b


def _ssd_chunked(xs, dt, bm, cm, a_log):
    f32 = jnp.float32
    b, L = xs.shape[:2]
    nc = L // CHUNK
    hpg = SSD_HEADS // SSD_GROUPS
    A = -jnp.exp(a_log.astype(f32))
    X = (xs * dt[..., None]).reshape(b, nc, CHUNK, SSD_GROUPS, hpg, SSD_HEADDIM)
    a = (dt * A).reshape(b, nc, CHUNK, SSD_GROUPS, hpg)
    bc = bm.reshape(b, nc, CHUNK, SSD_GROUPS, SSD_STATE)
    cc = cm.reshape(b, nc, CHUNK, SSD_GROUPS, SSD_STATE)
    a_cum = jnp.cumsum(a, axis=2)
    a_t = jnp.moveaxis(a_cum, 2, -1)
    causal = jnp.tril(jnp.ones((CHUNK, CHUNK), dtype=bool))
    decay = jnp.exp(jnp.where(causal, a_t[..., :, None] - a_t[..., None, :], -jnp.inf))
    cb = jnp.einsum("bclgn,bcsgn->bcgls", cc, bc)
    y_diag = jnp.einsum("bcgels,bcsgep->bclgep", cb[:, :, :, None] * decay, X)
    to_end = jnp.exp(a_cum[:, :, -1:] - a_cum)
    states = jnp.einsum("bclgn,bclgep->bcgepn", bc, X * to_end[..., None])
    chunk_decay = jnp.exp(a_cum[:, :, -1])

    def step(s, inp):
        st, dec = inp
        return dec[..., None, None] * s + st, s

    s0 = jnp.zeros((b, SSD_GROUPS, hpg, SSD_HEADDIM, SSD_STATE), f32)
    _, prev = lax.scan(step, s0, (jnp.moveaxis(states, 1, 0), jnp.moveaxis(chunk_decay, 1, 0)))
    y_off = jnp.einsum("bclgn,cbgepn->bclgep", cc, prev) * jnp.exp(a_cum)[..., None]
    return (y_diag + y_off).reshape(b, L, SSD_HEADS, SSD_HEADDIM)


def _ssd_mixer(h, w_in, conv_w, conv_b, dt_bias, a_log, d_skip, gnorm, w_out):
    f32 = jnp.float32
    b, L, _ = h.shape
    z, xbc, dt = _split(h @ w_in, (SSD_DI, SSD_CONV_DIM, SSD_HEADS))
    xbc = jax.nn.silu(_causal_dwconv(xbc, conv_w, conv_b))
    xs, bm, cm = _split(xbc.astype(f32), (SSD_DI, SSD_GROUPS * SSD_STATE, SSD_GROUPS * SSD_STATE))
    xs = xs.reshape(b, L, SSD_HEADS, SSD_HEADDIM)
    dt = jax.nn.softplus(dt.astype(f32) + dt_bias.astype(f32))
    y = _ssd_chunked(xs, dt, bm.reshape(b, L, SSD_GROUPS, SSD_STATE),
                     cm.reshape(b, L, SSD_GROUPS, SSD_STATE), a_log)
    y = y + d_skip.astype(f32)[:, None] * xs
    y = y.reshape(b, L, SSD_DI) * jax.nn.silu(z.astype(f32))
    y = _rmsnorm(y.reshape(b, L, SSD_GROUPS, SSD_DI // SSD_GROUPS),
                 gnorm.reshape(SSD_GROUPS, SSD_DI // SSD_GROUPS)).reshape(b, L, SSD_DI)
    return y.astype(h.dtype) @ w_out


def _retention_chunked(q, k, v):
    f32 = jnp.float32
    b, L = q.shape[:2]
    nc = L // CHUNK
    log_g = jnp.log1p(-jnp.exp2(-5.0 - jnp.arange(RET_HEADS, dtype=f32)))
    qc = q.astype(f32).reshape(b, nc, CHUNK, RET_HEADS, RET_DK)
    kc = k.astype(f32).reshape(b, nc, CHUNK, RET_HEADS, RET_DK)
    vc = v.astype(f32).reshape(b, nc, CHUNK, RET_HEADS, RET_DV)
    i = jnp.arange(CHUNK, dtype=f32)
    diff = i[:, None] - i[None, :]
    intra = jnp.where(diff >= 0, jnp.exp(jnp.maximum(diff, 0.0)[None] * log_g[:, None, None]), 0.0)
    scores = jnp.einsum("bclhd,bcshd->bchls", qc, kc) * intra
    o_intra = jnp.einsum("bchls,bcshe->bclhe", scores, vc)
    k_dec = kc * jnp.exp((CHUNK - 1 - i)[:, None] * log_g)[:, :, None]
    chunk_kv = jnp.einsum("bclhd,bclhe->bchde", k_dec, vc)
    chunk_decay = jnp.exp(CHUNK * log_g)

    def step(s, kv):
        return chunk_decay[:, None, None] * s + kv, s

    s0 = jnp.zeros((b, RET_HEADS, RET_DK, RET_DV), f32)
    _, prev = lax.scan(step, s0, jnp.moveaxis(chunk_kv, 1, 0))
    q_dec = qc * jnp.exp((i + 1.0)[:, None] * log_g)[:, :, None]
    o_cross = jnp.einsum("bclhd,cbhde->bclhe", q_dec, prev)
    return (o_intra + o_cross).reshape(b, L, RET_HEADS, RET_DV)


def _retention_mixer(h, positions, w_in, gnorm, w_out):
    f32 = jnp.float32
    b, L, _ = h.shape
    q, k, v, g = _split(h @ w_in, (RET_HEADS * RET_DK, RET_HEADS * RET_DK, RET_DVTOT, RET_DVTOT))
    inv = 1.0 / (jnp.float32(RET_THETA) ** jnp.linspace(0.0, 1.0, RET_DK // 2, dtype=f32))
    q = _rope(q.reshape(b, L, RET_HEADS, RET_DK), positions, inv)
    k = _rope(k.reshape(b, L, RET_HEADS, RET_DK), positions, inv) * (RET_DK ** -0.5)
    o = _retention_chunked(q, k, v.reshape(b, L, RET_HEADS, RET_DV))
    o = _rmsnorm(o, gnorm.reshape(RET_HEADS, RET_DV)).reshape(b, L, RET_DVTOT)
    o = o * jax.nn.silu(g.astype(f32))
    return o.astype(h.dtype) @ w_out


def _dsa_sparse_attention(q, k, v, qi, ki, wi, topk):
    f32 = jnp.float32
    L = q.shape[0]
    nb = L // Q_BLOCK
    key_pos = jnp.arange(L)
    kif = ki.astype(f32)

    def block(args):
        qb, qib, wb, j = args
        q_pos = j * Q_BLOCK + jnp.arange(Q_BLOCK)
        rel = jax.nn.relu(jnp.einsum("thd,sd->ths", qib.astype(f32), kif) * (IDX_DIM ** -0.5))
        score = jnp.einsum("th,ths->ts", wb.astype(f32), rel)
        score = jnp.where(key_pos[None, :] <= q_pos[:, None], score, -jnp.inf)
        _, sel = lax.top_k(score, topk)
        valid = sel <= q_pos[:, None]
        ks = k[sel].astype(f32)
        vs = v[sel].astype(f32)
        qg = qb.reshape(Q_BLOCK, DSA_KV_HEADS, DSA_GQA, DSA_HEADDIM).astype(f32)
        logits = jnp.einsum("tngd,tknd->tngk", qg, ks) * (DSA_HEADDIM ** -0.5)
        logits = jnp.where(valid[:, None, None, :], logits, -jnp.inf)
        p = jax.nn.softmax(logits, axis=-1)
        ob = jnp.einsum("tngk,tknd->tngd", p, vs)
        return ob.reshape(Q_BLOCK, DSA_HEADS, DSA_HEADDIM).astype(q.dtype)

    blocks = (q.reshape(nb, Q_BLOCK, DSA_HEADS, DSA_HEADDIM),
              qi.reshape(nb, Q_BLOCK, IDX_HEADS, IDX_DIM),
              wi.reshape(nb, Q_BLOCK, IDX_HEADS),
              jnp.arange(nb))
    return lax.map(block, blocks).reshape(L, DSA_HEADS, DSA_HEADDIM)


def _dsa_mixer(h, positions, w_in, idx_knorm, w_out):
    f32 = jnp.float32
    b, L, _ = h.shape
    q, k, v, g, qi, ki, wi = _split(h @ w_in, DSA_SPLITS)
    inv_a = _partial_inv_freq(DSA_HEADDIM)
    inv_i = _partial_inv_freq(IDX_DIM)
    q = _rope(q.reshape(b, L, DSA_HEADS, DSA_HEADDIM), positions, inv_a)
    k = _rope(k.reshape(b, L, DSA_KV_HEADS, DSA_HEADDIM), positions, inv_a)
    v = v.reshape(b, L, DSA_KV_HEADS, DSA_HEADDIM)
    qi = _rope(qi.reshape(b, L, IDX_HEADS, IDX_DIM), positions, inv_i)
    ki = _rope(_rmsnorm(ki, idx_knorm)[:, :, None, :], positions, inv_i)[:, :, 0, :]
    wi = wi * (IDX_HEADS ** -0.5)
    topk = min(TOPK_MAX, L // 4)
    o = jax.vmap(functools.partial(_dsa_sparse_attention, topk=topk))(q, k, v, qi, ki, wi)
    o = o.reshape(b, L, DSA_WIDTH).astype(f32) * jax.nn.silu(g.astype(f32))
    return o.astype(h.dtype) @ w_out


def _gain(key, n):
    return 1.0 + 0.02 * jax.random.normal(key, (n,), jnp.float32)


def _ssd_params(key, p):
    k = jax.random.split(key, 9)
    dt = jnp.exp(jax.random.uniform(k[3], (SSD_HEADS,), jnp.float32,
                                    minval=math.log(DT_MIN), maxval=math.log(DT_MAX)))
    return {
        p + "norm": _gain(k[0], D_MODEL),
        p + "w_in": jax.random.normal(k[1], (D_MODEL, SSD_IN), jnp.float32) * D_MODEL ** -0.5,
        p + "conv_w": jax.random.normal(k[2], (SSD_CONV, SSD_CONV_DIM), jnp.float32) * SSD_CONV ** -0.5,
        p + "conv_b": 0.02 * jax.random.normal(k[4], (SSD_CONV_DIM,), jnp.float32),
        p + "dt_bias": dt + jnp.log(-jnp.expm1(-dt)),
        p + "a_log": jnp.log(jax.random.uniform(k[5], (SSD_HEADS,), jnp.float32, minval=1.0, maxval=16.0)),
        p + "d_skip": _gain(k[6], SSD_HEADS),
        p + "gnorm": _gain(k[7], SSD_DI),
        p + "w_out": jax.random.normal(k[8], (SSD_DI, D_MODEL), jnp.float32) * SSD_DI ** -0.5 * OUT_SCALE,
    }


def _ret_params(key, p):
    k = jax.random.split(key, 4)
    return {
        p + "norm": _gain(k[0], D_MODEL),
        p + "w_in": jax.random.normal(k[1], (D_MODEL, RET_IN), jnp.float32) * D_MODEL ** -0.5,
        p + "gnorm": _gain(k[2], RET_DVTOT),
        p + "w_out": jax.random.normal(k[3], (RET_DVTOT, D_MODEL), jnp.float32) * RET_DVTOT ** -0.5 * OUT_SCALE,
    }


def _dsa_params(key, p):
    k = jax.random.split(key, 4)
    return {
        p + "norm": _gain(k[0], D_MODEL),
        p + "w_in": jax.random.normal(k[1], (D_MODEL, DSA_IN), jnp.float32) * D_MODEL ** -0.5,
        p + "idx_knorm": _gain(k[2], IDX_DIM),
        p + "w_out": jax.random.normal(k[3], (DSA_WIDTH, D_MODEL), jnp.float32) * DSA_WIDTH ** -0.5 * OUT_SCALE,
    }


def setup_inputs(seed: int = 0) -> dict:
    key = jax.random.key(seed)
    kx, k0, k1, k2, k3, kf = jax.random.split(key, 6)
    inputs = {
        "x": jax.random.normal(kx, (BATCH, SEQ, D_MODEL), jnp.float32),
        "positions": jnp.broadcast_to(jnp.arange(SEQ, dtype=jnp.int32), (BATCH, SEQ)),
    }
    inputs.update(_ssd_params(k0, "l0_"))
    inputs.update(_ret_params(k1, "l1_"))
    inputs.update(_dsa_params(k2, "l2_"))
    inputs.update(_ssd_params(k3, "l3_"))
    inputs["final_norm"] = _gain(kf, D_MODEL)
    return inputs


def reference(x, positions,
              l0_norm, l0_w_in, l0_conv_w, l0_conv_b, l0_dt_bias, l0_a_log, l0_d_skip, l0_gnorm, l0_w_out,
              l1_norm, l1_w_in, l1_gnorm, l1_w_out,
              l2_norm, l2_w_in, l2_idx_knorm, l2_w_out,
              l3_norm, l3_w_in, l3_conv_w, l3_conv_b, l3_dt_bias, l3_a_log, l3_d_skip, l3_gnorm, l3_w_out,
              final_norm):
    layer_params = [
        (l0_norm, (l0_w_in, l0_conv_w, l0_conv_b, l0_dt_bias, l0_a_log, l0_d_skip, l0_gnorm, l0_w_out)),
        (l1_norm, (l1_w_in, l1_gnorm, l1_w_out)),
        (l2_norm, (l2_w_in, l2_idx_knorm, l2_w_out)),
        (l3_norm, (l3_w_in, l3_conv_w, l3_conv_b, l3_dt_bias, l3_a_log, l3_d_skip, l3_gnorm, l3_w_out)),
    ]
    h = x
    for i in range(DEPTH):
        norm_g, p = layer_params[i]
        hn = _rmsnorm(h, norm_g)
        kind = i % N_MIXERS
        if kind == 0:
            y = _ssd_mixer(hn, *p)
        elif kind == 1:
            y = _retention_mixer(hn, positions, *p)
        else:
            y = _dsa_mixer(hn, positions, *p)
        h = h + y
    return _rmsnorm(h, final_norm)
```

```python
import numpy as np
import concourse.bass as bass
import concourse.mybir as mybir
from concourse.bass_utils import run_bass_kernel_spmd

F32 = mybir.dt.float32
BF16 = mybir.dt.bfloat16
I32 = mybir.dt.int32
U8 = mybir.dt.uint8
AF = mybir.ActivationFunctionType
ALU = mybir.AluOpType
AX = mybir.AxisListType

NDS = 24


class Buf:
    __slots__ = ("name", "lastw", "readers", "psum")

    def __init__(self, name, psum=False):
        self.name = name
        self.lastw = None
        self.readers = {}
        self.psum = psum


class Prog:
    def __init__(self, nc):
        self.nc = nc
        self.E = {"pe": nc.tensor, "dve": nc.vector, "act": nc.scalar,
                  "pool": nc.gpsimd, "sp": nc.sync}
        self.csem = {k: nc.alloc_semaphore("cs_" + k) for k in ("pe", "dve", "act", "pool")}
        self.ccnt = {k: 0 for k in self.csem}
        self.dsems = [nc.alloc_semaphore("ds%d" % i) for i in range(NDS)]
        self.dcnt = [0] * NDS
        self.dnext = 0
        self.xsems = []
        self.seen = {k: {} for k in self.E}
        self.pend_r = {k: [] for k in self.csem}
        self.pend_w = {k: [] for k in self.csem}
        self.nbuf = 0
        self.ninst = 0

    def sb(self, name, shape, dtype=F32):
        t = self.nc.alloc_sbuf_tensor(name, list(shape), dtype)
        return t.ap(), Buf(name)

    def ps(self, name, shape, dtype=F32):
        t = self.nc.alloc_psum_tensor(name, list(shape), dtype)
        return t.ap(), Buf(name, psum=True)

    def buf(self, name="b"):
        self.nbuf += 1
        return Buf("%s%d" % (name, self.nbuf))

    def _wait(self, eng, tok):
        kind, key, val = tok
        k = (kind, key)
        if self.seen[eng].get(k, 0) >= val:
            return
        sem = self.csem[key] if kind == "c" else (self.dsems[key] if kind == "d" else self.xsems[key])
        self.E[eng].wait_ge(sem, val)
        self.seen[eng][k] = val

    def _deps(self, eng, reads, writes):
        toks = []
        for b in reads:
            if b.lastw is not None:
                toks.append(b.lastw)
            if b.psum:
                for e2, t in b.readers.items():
                    if e2 != eng:
                        toks.append(t)
        for b in writes:
            if b.lastw is not None and not (eng == "pe" and b.lastw[0] == "c" and b.lastw[1] == eng):
                toks.append(b.lastw)
            for e2, t in b.readers.items():
                toks.append(t)
        for t in toks:
            self._wait(eng, t)

    def op(self, eng, fn, reads=(), writes=(), inc=True):
        for b in list(reads) + list(writes):
            for e2 in self.csem:
                if e2 != eng:
                    assert b not in self.pend_r[e2] and b not in self.pend_w[e2], \
                        "buffer %s used while pending on %s" % (b.name, e2)
        self._deps(eng, reads, writes)
        inst = fn(self.E[eng])
        self.ninst += 1
        if not inc:
            self.pend_r[eng].extend(reads)
            self.pend_w[eng].extend(writes)
            return inst
        self.ccnt[eng] += 1
        inst.then_inc(self.csem[eng], 1)
        tok = ("c", eng, self.ccnt[eng])
        for b in list(reads) + self.pend_r[eng]:
            b.readers[eng] = tok
        for b in list(writes) + self.pend_w[eng]:
            b.lastw = tok
            b.readers = {}
        self.pend_r[eng] = []
        self.pend_w[eng] = []
        return inst

    def dma(self, q, out, in_, reads=(), writes=(), **kw):
        for b in list(reads) + list(writes):
            for e2 in self.csem:
                assert b not in self.pend_r[e2] and b not in self.pend_w[e2]
        self._deps(q, reads, writes)
        if q == "pool":
            self.xsems.append(self.nc.alloc_semaphore("xs%d" % len(self.xsems)))
            inst = self.E[q].dma_start(out=out, in_=in_, **kw)
            self.ninst += 1
            inst.then_inc(self.xsems[-1], 16)
            tok = ("x", len(self.xsems) - 1, 16)
            for b in reads:
                b.readers[("q", q, "x%d" % len(self.xsems))] = tok
            for b in writes:
                b.lastw = tok
                b.readers = {}
            return tok
        s = self.dnext % NDS
        self.dnext += 1
        if self.dcnt[s] > 0:
            self._wait(q, ("d", s, self.dcnt[s]))
        inst = self.E[q].dma_start(out=out, in_=in_, **kw)
        self.ninst += 1
        self.dcnt[s] += 16
        inst.then_inc(self.dsems[s], 16)
        tok = ("d", s, self.dcnt[s])
        for b in reads:
            b.readers[("q", q, s)] = tok
        for b in writes:
            b.lastw = tok
            b.readers = {}
        return tok

    def finish(self):
        for s in range(NDS):
            if self.dcnt[s] > 0:
                self._wait("sp", ("d", s, self.dcnt[s]))
        for i in range(len(self.xsems)):
            self._wait("sp", ("x", i, 16))
        for e in self.csem:
            if self.ccnt[e] > 0:
                self._wait("sp", ("c", e, self.ccnt[e]))
NT = 2048
D = 1024
EPS = 1e-6
TWO_PI = 6.283185307179586
CW1 = 6.28125
CW2 = TWO_PI - CW1
MAGIC = 12582912.0
PI_SAFE = 3.1415925


def emit_sincos(P, r_in, rb, sin_out, sinb, cos_out, cosb, tmp, tmpb, tmp2, tmp2b):
    P.op("dve", lambda e: e.tensor_scalar(out=tmp, in0=r_in, scalar1=1.0 / TWO_PI, scalar2=MAGIC,
                                          op0=ALU.mult, op1=ALU.add), reads=[rb], writes=[tmpb])
    P.op("dve", lambda e: e.tensor_scalar(out=tmp, in0=tmp, scalar1=-MAGIC, scalar2=None, op0=ALU.add),
         reads=[tmpb], writes=[tmpb])
    P.op("dve", lambda e: e.scalar_tensor_tensor(out=tmp2, in0=tmp, scalar=-CW1, in1=r_in,
                                                 op0=ALU.mult, op1=ALU.add), reads=[tmpb, rb], writes=[tmp2b])
    P.op("dve", lambda e: e.scalar_tensor_tensor(out=r_in, in0=tmp, scalar=-CW2, in1=tmp2,
                                                 op0=ALU.mult, op1=ALU.add), reads=[tmpb, tmp2b], writes=[rb])
    P.op("dve", lambda e: e.tensor_scalar(out=r_in, in0=r_in, scalar1=-PI_SAFE, scalar2=PI_SAFE,
                                          op0=ALU.max, op1=ALU.min), reads=[rb], writes=[rb])
    P.op("act", lambda e: e.activation(out=sin_out, in_=r_in, func=AF.Sin), reads=[rb], writes=[sinb])
    P.op("dve", lambda e: e.tensor_scalar(out=tmp, in0=r_in, scalar1=1.5707963267948966, scalar2=-TWO_PI,
                                          op0=ALU.is_gt, op1=ALU.mult), reads=[rb], writes=[tmpb])
    P.op("dve", lambda e: e.tensor_tensor(out=tmp2, in0=tmp, in1=r_in, op=ALU.add), reads=[tmpb, rb], writes=[tmp2b])
    P.op("dve", lambda e: e.tensor_scalar(out=tmp2, in0=tmp2, scalar1=1.5707963267948966, scalar2=PI_SAFE,
                                          op0=ALU.add, op1=ALU.min), reads=[tmp2b], writes=[tmp2b])
    P.op("act", lambda e: e.activation(out=cos_out, in_=tmp2, func=AF.Sin), reads=[tmp2b], writes=[cosb])


def build_tok(act, proj):
    nc = bass.Bass("TRN2", target_bir_lowering=False)
    P = Prog(nc)

    def din(name, shape, dt=F32):
        return nc.dram_tensor(name, list(shape), dt, kind="ExternalInput").ap()

    def dout(name, shape, dt=F32):
        return nc.dram_tensor(name, list(shape), dt, kind="ExternalOutput").ap()

    Ka = {None: 0, "ssd": 2048, "ret": 2048, "dsa": 1024}[act]
    N = {"ssd": 6176, "ret": 6144, "dsa": 3664, "final": 0}[proj]
    h_in = din("h_in", [NT, D])
    ident_d = din("ident", [128, 128], BF16)
    norm_g = din("norm_g", [1, D])
    if act:
        a_in = din("a_in", [NT, Ka], BF16)
        w_out = din("w_out", [Ka, D])
        h_out = dout("h_out", [NT, D])
        if act in ("ret", "dsa"):
            sg_in = din("sg_in", [NT, Ka], BF16)
        if act == "ret":
            gn_d = din("gn", [1, Ka])
    if proj != "final":
        w_in = din("w_in", [D, N])
    if proj == "ssd":
        z_o = dout("z", [NT, 2048], BF16)
        xbcT_o = dout("xbcT", [4096, NT], BF16)
        dt_o = dout("dt", [NT, 32])
    elif proj == "ret":
        pos_d = din("pos", [1, NT], I32)
        inv_d = din("inv", [128, 1])
        qT_o = dout("qT", [1024, NT], BF16)
        kT_o = dout("kT", [1024, NT], BF16)
        v_o = dout("v", [NT, 2048], BF16)
        sg_o = dout("sg", [NT, 2048], BF16)
    elif proj == "dsa":
        pos_d = din("pos", [NT, 1], I32)
        inv_d = din("inv", [1, 8])
        knorm_d = din("knorm", [1, 64])
        qT_o = dout("qT", [1024, NT], BF16)
        kT_o = dout("kT", [256, NT], BF16)
        v_o = dout("v", [NT, 256], BF16)
        sg_o = dout("sg", [NT, 1024], BF16)
        qiT_o = dout("qiT", [1024, NT], BF16)
        kiT_o = dout("kiT", [64, NT], BF16)
        wi_o = dout("wi", [NT, 16])
    else:
        out_o = dout("out", [NT, D])

    ident, identb = P.sb("ident_sb", [128, 128], BF16)
    P.dma("sp", ident, ident_d, writes=[identb])
    gt, gtb = P.sb("gt", [128, D])
    P.dma("sp", gt, norm_g.partition_broadcast(128), writes=[gtb])
    if proj != "final":
        w_sb, w_b = P.sb("w_sb", [128, 8, N], BF16)
        w_bufs = [P.buf("w") for _ in range((N + 511) // 512)]
    if act:
        KA = Ka // 128
        wo_sb, wo_b = P.sb("wo_sb", [128, KA, D], BF16)
        wo_bufs = [P.buf("wo"), P.buf("wo")]
        for nb in range(2):
            P.dma("pool", wo_sb[:, :, nb * 512:(nb + 1) * 512],
                  w_out[:, nb * 512:(nb + 1) * 512].rearrange("(k p) n -> p k n", p=128), writes=[wo_bufs[nb]])
        a_t, a_b = P.sb("a_t", [128, Ka], BF16)
        aT, aT_b = P.sb("aT", [128, KA, 128], BF16)
        if act in ("ret", "dsa"):
            sg_t, sg_b = P.sb("sg_t", [128, Ka], BF16)
            ap_t, ap_b = P.sb("ap_t", [128, Ka], BF16)
        if act == "ret":
            gn_t, gn_b = P.sb("gn_t", [128, Ka])
            P.dma("sp", gn_t, gn_d.partition_broadcast(128), writes=[gn_b])
            ss4, ss4_b = P.sb("ss4", [128, 4])
            an_t, an_b = P.sb("an_t", [128, 512])
    if proj != "final":
        for c0 in range(0, N, 512):
            c1 = min(N, c0 + 512)
            P.dma("pool", w_sb[:, :, c0:c1], w_in[:, c0:c1].rearrange("(k p) n -> p k n", p=128), writes=[w_bufs[c0 // 512]])
    h_t, h_b = P.sb("h_t", [128, D])
    sq_t, sq_b = P.sb("sq_t", [128, D])
    ss, ss_b = P.sb("ss", [128, 1])
    if proj != "final":
        hn, hn_b = P.sb("hn", [128, D], BF16)
        hnT, hnT_b = P.sb("hnT", [128, 8, 512], BF16)
    NSTG = 3
    stg = [P.sb("stg%d" % i, [128, 512]) for i in range(NSTG)]
    stgh = [P.sb("stgh%d" % i, [128, 512], BF16) for i in range(NSTG)]
    stg_i = [0, 0]

    def next_stg(half):
        lst = stgh if half else stg
        i = stg_i[half] % NSTG
        stg_i[half] += 1
        return lst[i]

    pT = [P.ps("pT%d" % i, [128, 8, 128], BF16) for i in range(2)]
    pO, pO_b = P.ps("pO", [128, 1024])
    pP = [P.ps("pP%d" % i, [128, 512]) for i in range(4)]
    cnt = {"pT": 0, "pP": 0, "ev": 0}

    def next_pT():
        cnt["pT"] += 1
        return pT[cnt["pT"] % 2]

    def next_pP():
        cnt["pP"] += 1
        return pP[cnt["pP"] % 4]

    def evac_engine():
        cnt["ev"] += 1
        return "act" if cnt["ev"] % 2 else "dve"

    def copy(eng, out, in_, reads, writes):
        if eng == "act":
            P.op("act", lambda e: e.copy(out=out, in_=in_), reads=reads, writes=writes)
        else:
            P.op(eng, lambda e: e.tensor_copy(out=out, in_=in_), reads=reads, writes=writes)

    def transpose_blocks(src, src_b, nblk, dst, dst_b, dst_off=0):
        j = 0
        while j < nblk:
            n = min(8, nblk - j)
            pt, ptb = next_pT()
            for i in range(n):
                P.op("pe", lambda e: e.transpose(out=pt[:, i, :], in_=src[:, (j + i) * 128:(j + i + 1) * 128],
                                                 identity=ident),
                     reads=[src_b, identb], writes=[ptb], inc=(i == n - 1))
            copy(evac_engine(), dst[:, dst_off + j:dst_off + j + n, :], pt[:, 0:n, :], [ptb], [dst_b])
            j += n

    if proj == "ret":
        inv_c, inv_b = P.sb("inv_c", [128, 1])
        P.dma("sp", inv_c, inv_d, writes=[inv_b])
        posi, posi_b = P.sb("posi", [128, 512], I32)
        ang, ang_b = P.sb("ang", [128, 512])
        tA, tA_b = P.sb("tA", [128, 512])
        tB, tB_b = P.sb("tB", [128, 512])
        sinT, sinT_b = P.sb("sinT", [128, 512])
        cosT, cosT_b = P.sb("cosT", [128, 512])
        sinK, sinK_b = P.sb("sinK", [128, 512])
        cosK, cosK_b = P.sb("cosK", [128, 512])
    if proj == "dsa":
        inv8, inv8_b = P.sb("inv8", [128, 8])
        P.dma("sp", inv8, inv_d.partition_broadcast(128), writes=[inv8_b])
        kn_t, kn_b = P.sb("kn_t", [128, 64])
        P.dma("sp", kn_t, knorm_d.partition_broadcast(128), writes=[kn_b])
        posi, posi_b = P.sb("posi", [128, 1], I32)
        posf, posf_b = P.sb("posf", [128, 1])
        ang, ang_b = P.sb("ang", [128, 8])
        tA, tA_b = P.sb("tA", [128, 8])
        tB, tB_b = P.sb("tB", [128, 8])
        sin8, sin8_b = P.sb("sin8", [128, 8])
        cos8, cos8_b = P.sb("cos8", [128, 8])
        r1, r1_b = P.sb("r1", [128, 64])
        r2, r2_b = P.sb("r2", [128, 64])
        fT = {}
        for nm, nb in (("q", 8), ("k", 2), ("qi", 8), ("ki", 1)):
            fT[nm] = P.sb("fT_" + nm, [128, nb, 512], BF16)
        kis, kis_b = P.sb("kis", [128, 128], BF16)
        P.op("pool", lambda e: e.memset(kis, 0.0), writes=[kis_b])
        ki32, ki32_b = P.sb("ki32", [128, 64])
        wi32, wi32_b = P.sb("wi32", [128, 16])

    for st in range(NT // 512):
        for sub in range(4):
            t0 = st * 512 + sub * 128
            P.dma("sp", h_t, h_in[t0:t0 + 128, :], writes=[h_b])
            if act:
                P.dma("sp", a_t, a_in[t0:t0 + 128, :], writes=[a_b])
                src, src_b = a_t, a_b
                if act in ("ret", "dsa"):
                    P.dma("sp", sg_t, sg_in[t0:t0 + 128, :], writes=[sg_b])
                if act == "dsa":
                    P.op("dve", lambda e: e.tensor_tensor(out=ap_t, in0=a_t, in1=sg_t, op=ALU.mult),
                         reads=[a_b, sg_b], writes=[ap_b])
                    src, src_b = ap_t, ap_b
                if act == "ret":
                    for hh in range(4):
                        P.op("act", lambda e: e.activation(out=sq_t[:, 0:512], in_=a_t[:, hh * 512:(hh + 1) * 512],
                                                           func=AF.Square, accum_out=ss4[:, hh:hh + 1]),
                             reads=[a_b], writes=[sq_b, ss4_b])
                    P.op("act", lambda e: e.activation(out=ss4, in_=ss4, func=AF.Sqrt, scale=1.0 / 512, bias=EPS),
                         reads=[ss4_b], writes=[ss4_b])
                    P.op("dve", lambda e: e.reciprocal(out=ss4, in_=ss4), reads=[ss4_b], writes=[ss4_b])
                    for hh in range(4):
                        sl = slice(hh * 512, (hh + 1) * 512)
                        P.op("dve", lambda e: e.scalar_tensor_tensor(out=an_t, in0=a_t[:, sl], scalar=ss4[:, hh:hh + 1],
                                                                     in1=gn_t[:, sl], op0=ALU.mult, op1=ALU.mult),
                             reads=[a_b, ss4_b, gn_b], writes=[an_b])
                        P.op("dve", lambda e: e.tensor_tensor(out=ap_t[:, sl], in0=an_t, in1=sg_t[:, sl], op=ALU.mult),
                             reads=[an_b, sg_b], writes=[ap_b])
                    src, src_b = ap_t, ap_b
                transpose_blocks(src, src_b, KA, aT, aT_b)
                for nb in range(2):
                    for k in range(KA):
                        P.op("pe", lambda e: e.matmul(pO[:, nb * 512:(nb + 1) * 512], lhsT=aT[:, k, :],
                                                      rhs=wo_sb[:, k, nb * 512:(nb + 1) * 512],
                                                      start=(k == 0), stop=(k == KA - 1)),
                             reads=[aT_b, wo_bufs[nb]], writes=[pO_b], inc=(nb == 1 and k == KA - 1))
                P.op("dve", lambda e: e.tensor_tensor(out=h_t, in0=pO, in1=h_t, op=ALU.add),
                     reads=[pO_b, h_b], writes=[h_b])
                P.dma("sp", h_out[t0:t0 + 128, :], h_t, reads=[h_b])
            P.op("act", lambda e: e.activation(out=sq_t, in_=h_t, func=AF.Square, accum_out=ss),
                 reads=[h_b], writes=[sq_b, ss_b])
            P.op("act", lambda e: e.activation(out=ss, in_=ss, func=AF.Sqrt, scale=1.0 / D, bias=EPS),
                 reads=[ss_b], writes=[ss_b])
            P.op("dve", lambda e: e.reciprocal(out=ss, in_=ss), reads=[ss_b], writes=[ss_b])
            if proj == "final":
                P.op("dve", lambda e: e.scalar_tensor_tensor(out=sq_t, in0=h_t, scalar=ss, in1=gt,
                                                             op0=ALU.mult, op1=ALU.mult),
                     reads=[h_b, ss_b, gtb], writes=[sq_b])
                P.dma("sp", out_o[t0:t0 + 128, :], sq_t, reads=[sq_b])
                continue
            P.op("dve", lambda e: e.scalar_tensor_tensor(out=hn, in0=h_t, scalar=ss, in1=gt,
                                                         op0=ALU.mult, op1=ALU.mult),
                 reads=[h_b, ss_b, gtb], writes=[hn_b])
            pt, ptb = next_pT()
            for k in range(8):
                P.op("pe", lambda e: e.transpose(out=pt[:, k, :], in_=hn[:, k * 128:(k + 1) * 128], identity=ident),
                     reads=[hn_b, identb], writes=[ptb], inc=(k == 7))
            copy(evac_engine(), hnT[:, :, sub * 128:(sub + 1) * 128], pt, [ptb], [hnT_b])
        if proj == "final":
            continue
        T0 = st * 512

        def mm_tok(sub, c0, ncol):
            pp, ppb = next_pP()
            for k in range(8):
                P.op("pe", lambda e: e.matmul(pp[:, 0:ncol], lhsT=hnT[:, k, sub * 128:(sub + 1) * 128],
                                              rhs=w_sb[:, k, c0:c0 + ncol], start=(k == 0), stop=(k == 7)),
                     reads=[hnT_b, w_bufs[c0 // 512]], writes=[ppb], inc=(k == 7))
            return pp, ppb

        def mm_feat(f0):
            pp, ppb = next_pP()
            for k in range(8):
                P.op("pe", lambda e: e.matmul(pp, lhsT=w_sb[:, k, f0:f0 + 128], rhs=hnT[:, k, :],
                                              start=(k == 0), stop=(k == 7)),
                     reads=[hnT_b, w_bufs[f0 // 512]], writes=[ppb], inc=(k == 7))
            return pp, ppb

        def tok_block_out(c0, ncol, dst, dcol, func=None, fp32=False):
            for sub in range(4):
                pp, ppb = mm_tok(sub, c0, ncol)
                s, sb_ = next_stg(0 if fp32 else 1)
                if func is None:
                    copy(evac_engine(), s[:, 0:ncol], pp[:, 0:ncol], [ppb], [sb_])
                else:
                    P.op("act", lambda e: e.activation(out=s[:, 0:ncol], in_=pp[:, 0:ncol], func=func),
                         reads=[ppb], writes=[sb_])
                r0 = T0 + sub * 128
                P.dma("sp", dst[r0:r0 + 128, dcol:dcol + ncol], s[:, 0:ncol], reads=[sb_])

        if proj == "ssd":
            for blk in range(4):
                tok_block_out(blk * 512, 512, z_o, blk * 512)
            for fb in range(32):
                pp, ppb = mm_feat(2048 + fb * 128)
                s, sb_ = next_stg(1)
                copy(evac_engine(), s, pp, [ppb], [sb_])
                P.dma("sp", xbcT_o[fb * 128:(fb + 1) * 128, T0:T0 + 512], s, reads=[sb_])
            tok_block_out(6144, 32, dt_o, 0, fp32=True)
        elif proj == "ret":
            P.dma("sp", posi, pos_d[:, T0:T0 + 512].partition_broadcast(128), writes=[posi_b])
            P.op("dve", lambda e: e.tensor_copy(out=ang, in_=posi), reads=[posi_b], writes=[ang_b])
            P.op("dve", lambda e: e.tensor_scalar(out=ang, in0=ang, scalar1=inv_c, scalar2=None, op0=ALU.mult),
                 reads=[ang_b, inv_b], writes=[ang_b])
            emit_sincos(P, ang, ang_b, sinT, sinT_b, cosT, cosT_b, tA, tA_b, tB, tB_b)
            P.op("pool", lambda e: e.tensor_scalar(out=sinK, in0=sinT, scalar1=0.0625, scalar2=None, op0=ALU.mult),
                 reads=[sinT_b], writes=[sinK_b])
            P.op("pool", lambda e: e.tensor_scalar(out=cosK, in0=cosT, scalar1=0.0625, scalar2=None, op0=ALU.mult),
                 reads=[cosT_b], writes=[cosK_b])
            for which, base, dst in (("q", 0, qT_o), ("k", 1024, kT_o)):
                cs, cs_b, sn, sn_b = (cosT, cosT_b, sinT, sinT_b) if which == "q" else (cosK, cosK_b, sinK, sinK_b)
                for hh in range(4):
                    p1, p1b = mm_feat(base + hh * 256)
                    p2, p2b = mm_feat(base + hh * 256 + 128)
                    o1, o1b = next_stg(1)
                    o2, o2b = next_stg(1)
                    P.op("dve", lambda e: e.tensor_tensor(out=tA, in0=p1, in1=cs, op=ALU.mult), reads=[p1b, cs_b], writes=[tA_b])
                    P.op("dve", lambda e: e.tensor_tensor(out=tB, in0=p2, in1=sn, op=ALU.mult), reads=[p2b, sn_b], writes=[tB_b])
                    P.op("pool", lambda e: e.tensor_tensor(out=o1, in0=tA, in1=tB, op=ALU.subtract), reads=[tA_b, tB_b], writes=[o1b])
                    P.op("dve", lambda e: e.tensor_tensor(out=ang, in0=p2, in1=cs, op=ALU.mult), reads=[p2b, cs_b], writes=[ang_b])
                    P.op("dve", lambda e: e.tensor_tensor(out=sq_t[:, 0:512], in0=p1, in1=sn, op=ALU.mult), reads=[p1b, sn_b], writes=[sq_b])
                    P.op("pool", lambda e: e.tensor_tensor(out=o2, in0=ang, in1=sq_t[:, 0:512], op=ALU.add), reads=[ang_b, sq_b], writes=[o2b])
                    f0 = hh * 256
                    P.dma("sp", dst[f0:f0 + 128, T0:T0 + 512], o1, reads=[o1b])
                    P.dma("sp", dst[f0 + 128:f0 + 256, T0:T0 + 512], o2, reads=[o2b])
            for blk in range(4):
                tok_block_out(2048 + blk * 512, 512, v_o, blk * 512)
            for blk in range(4):
                tok_block_out(4096 + blk * 512, 512, sg_o, blk * 512, func=AF.Silu)
        elif proj == "dsa":
            for sub in range(4):
                r0 = T0 + sub * 128
                P.dma("sp", posi, pos_d[r0:r0 + 128, :], writes=[posi_b])
                P.op("dve", lambda e: e.tensor_copy(out=posf, in_=posi), reads=[posi_b], writes=[posf_b])
                P.op("dve", lambda e: e.tensor_scalar(out=ang, in0=inv8, scalar1=posf, scalar2=None, op0=ALU.mult),
                     reads=[inv8_b, posf_b], writes=[ang_b])
                emit_sincos(P, ang, ang_b, sin8, sin8_b, cos8, cos8_b, tA, tA_b, tB, tB_b)

                def rope_tok(pp, ppb, nh, s, sb_):
                    p3 = pp[:, 0:nh * 64].rearrange("p (h d) -> p h d", d=64)
                    s3 = s[:, 0:nh * 64].rearrange("p (h d) -> p h d", d=64)
                    cb = cos8[:, None, :].to_broadcast([128, nh, 8])
                    sb2 = sin8[:, None, :].to_broadcast([128, nh, 8])
                    a3 = r1[:, 0:nh * 8].rearrange("p (h d) -> p h d", d=8)
                    b3 = r2[:, 0:nh * 8].rearrange("p (h d) -> p h d", d=8)
                    P.op("act", lambda e: e.copy(out=s[:, 0:nh * 64], in_=pp[:, 0:nh * 64]), reads=[ppb], writes=[sb_])
                    P.op("dve", lambda e: e.tensor_tensor(out=a3, in0=p3[:, :, 0:8], in1=cb, op=ALU.mult), reads=[ppb, cos8_b], writes=[r1_b])
                    P.op("dve", lambda e: e.tensor_tensor(out=b3, in0=p3[:, :, 8:16], in1=sb2, op=ALU.mult), reads=[ppb, sin8_b], writes=[r2_b])
                    P.op("dve", lambda e: e.tensor_tensor(out=s3[:, :, 0:8], in0=a3, in1=b3, op=ALU.subtract), reads=[r1_b, r2_b], writes=[sb_])
                    P.op("dve", lambda e: e.tensor_tensor(out=a3, in0=p3[:, :, 8:16], in1=cb, op=ALU.mult), reads=[ppb, cos8_b], writes=[r1_b])
                    P.op("dve", lambda e: e.tensor_tensor(out=b3, in0=p3[:, :, 0:8], in1=sb2, op=ALU.mult), reads=[ppb, sin8_b], writes=[r2_b])
                    P.op("dve", lambda e: e.tensor_tensor(out=s3[:, :, 8:16], in0=a3, in1=b3, op=ALU.add), reads=[r1_b, r2_b], writes=[sb_])

                for nm, cbase in (("q", 0), ("qi", 2560)):
                    ft, ftb = fT[nm]
                    for blk in range(2):
                        pp, ppb = mm_tok(sub, cbase + blk * 512, 512)
                        s, sb_ = next_stg(1)
                        rope_tok(pp, ppb, 8, s, sb_)
                        pt, ptb = next_pT()
                        for i in range(4):
                            P.op("pe", lambda e: e.transpose(out=pt[:, i, :], in_=s[:, i * 128:(i + 1) * 128], identity=ident),
                                 reads=[sb_, identb], writes=[ptb], inc=(i == 3))
                        copy(evac_engine(), ft[:, blk * 4:blk * 4 + 4, sub * 128:(sub + 1) * 128], pt[:, 0:4, :], [ptb], [ftb])
                pp, ppb = mm_tok(sub, 1024, 512)
                s, sb_ = next_stg(1)
                rope_tok(pp, ppb, 4, s, sb_)
                ft, ftb = fT["k"]
                pt, ptb = next_pT()
                for i in range(2):
                    P.op("pe", lambda e: e.transpose(out=pt[:, i, :], in_=s[:, i * 128:(i + 1) * 128], identity=ident),
                         reads=[sb_, identb], writes=[ptb], inc=(i == 1))
                copy(evac_engine(), ft[:, 0:2, sub * 128:(sub + 1) * 128], pt[:, 0:2, :], [ptb], [ftb])
                s2, s2b = next_stg(1)
                copy(evac_engine(), s2[:, 0:256], pp[:, 256:512], [ppb], [s2b])
                P.dma("sp", v_o[r0:r0 + 128, :], s2[:, 0:256], reads=[s2b])
                pp, ppb = mm_tok(sub, 3584, 80)
                P.op("dve", lambda e: e.tensor_scalar(out=wi32, in0=pp[:, 64:80], scalar1=1.0 / 32, scalar2=None, op0=ALU.mult),
                     reads=[ppb], writes=[wi32_b])
                P.dma("sp", wi_o[r0:r0 + 128, :], wi32, reads=[wi32_b])
                P.op("act", lambda e: e.activation(out=sq_t[:, 0:64], in_=pp[:, 0:64], func=AF.Square, accum_out=ss),
                     reads=[ppb], writes=[sq_b, ss_b])
                P.op("act", lambda e: e.activation(out=ss, in_=ss, func=AF.Sqrt, scale=1.0 / 64, bias=EPS), reads=[ss_b], writes=[ss_b])
                P.op("dve", lambda e: e.reciprocal(out=ss, in_=ss), reads=[ss_b], writes=[ss_b])
                P.op("dve", lambda e: e.scalar_tensor_tensor(out=ki32, in0=pp[:, 0:64], scalar=ss, in1=kn_t, op0=ALU.mult, op1=ALU.mult),
                     reads=[ppb, ss_b, kn_b], writes=[ki32_b])
                rope_tok(ki32, ki32_b, 1, kis, kis_b)
                ft, ftb = fT["ki"]
                pt, ptb = next_pT()
                P.op("pe", lambda e: e.transpose(out=pt[:, 0, :], in_=kis, identity=ident), reads=[kis_b, identb], writes=[ptb])
                copy(evac_engine(), ft[:, 0:1, sub * 128:(sub + 1) * 128], pt[:, 0:1, :], [ptb], [ftb])
            for nm, dst, nb in (("q", qT_o, 8), ("qi", qiT_o, 8), ("k", kT_o, 2)):
                ft, ftb = fT[nm]
                P.dma("sp", dst[:, T0:T0 + 512].rearrange("(b p) t -> p b t", p=128), ft, reads=[ftb])
            ft, ftb = fT["ki"]
            P.dma("sp", kiT_o[:, T0:T0 + 512], ft[0:64, 0, :], reads=[ftb])
            for blk in range(2):
                tok_block_out(1536 + blk * 512, 512, sg_o, blk * 512, func=AF.Silu)
    P.finish()
    return nc
L_SEQ = 16384
NCH = L_SEQ // 128


def build_ssd(nchunks=NCH):
    nc = bass.Bass("TRN2", target_bir_lowering=False)
    P = Prog(nc)
    Lc = nchunks * 128

    def din(name, shape, dt=F32):
        return nc.dram_tensor(name, list(shape), dt, kind="ExternalInput").ap()

    xbcT = din("xbcT", [512, Lc], BF16)
    z_d = din("z", [Lc, 256], BF16)
    dt_d = din("dt", [128, nchunks * 4])
    cw_d = din("cw", [128, 16])
    cb_d = din("cb", [128, 4])
    dtb_d = din("dtb", [1, 4])
    alog_d = din("alog", [1, 4])
    dsk_d = din("dsk", [1, 4])
    gn_d = din("gn", [1, 256])
    ident_d = din("ident", [128, 128], BF16)
    triu_d = din("triu", [128, 128])
    ones_d = din("ones", [128, 128])
    negm_d = din("negm", [128, 128], BF16)
    sel_d = din("sel", [128, 4 * 128])
    nsel_d = din("nsel", [128, 4 * 128])
    y_o = nc.dram_tensor("y", [Lc, 256], BF16, kind="ExternalOutput").ap()

    def ld(name, shape, src, dt=F32):
        t, b = P.sb(name, shape, dt)
        P.dma("sp", t, src, writes=[b])
        return t, b

    ident, ident_b = ld("ident_s", [128, 128], ident_d, BF16)
    triu, triu_b = ld("triu_s", [128, 128], triu_d)
    ones, ones_b = ld("ones_s", [128, 128], ones_d)
    negm, negm_b = ld("negm_s", [128, 128], negm_d, BF16)
    sel, sel_b = ld("sel_s", [128, 512], sel_d)
    nsel, nsel_b = ld("nsel_s", [128, 512], nsel_d)
    cw2, cw_b = ld("cw_s", [128, 16], cw_d)
    cw = cw2.rearrange("p (k j) -> p k j", j=4)
    cb2, cb_b = ld("cb_s", [128, 4], cb_d)
    cb = cb2.rearrange("p (k j) -> p k j", j=1)
    dtb, dtb_b = ld("dtb_s", [128, 4], dtb_d.partition_broadcast(128))
    alog, alog_b = ld("alog_s", [128, 4], alog_d.partition_broadcast(128))
    dsk, dsk_b = ld("dsk_s", [128, 4], dsk_d.partition_broadcast(128))
    gn, gn_b = ld("gn_s", [128, 256], gn_d.partition_broadcast(128))
    NC4 = nchunks * 4
    dts2, dts_b = ld("dts", [128, nchunks * 4], dt_d)
    dts = dts2.rearrange("p (c h) -> p c h", h=4)

    tm, tm_b = P.sb("tm", [128, NC4])
    te, te_b = P.sb("te", [128, NC4])
    aalp, aal_b = P.sb("aal", [128, NC4 + 128])
    P.op("pool", lambda e: e.memset(aalp, 0.0), writes=[aal_b])
    aal = aalp[:, 0:NC4].rearrange("p (c h) -> p c h", h=4)
    acum, acum_b = P.sb("acum", [128, nchunks, 4])
    eac, eac_b = P.sb("eac", [128, nchunks, 4])
    dtw, dtw_b = P.sb("dtw", [128, nchunks, 4])
    cdec, cdec_b = P.sb("cdec", [128, nchunks, 4])
    Aneg, Aneg_b = P.sb("Aneg", [128, 4])
    aal2 = aalp
    acum2 = acum.rearrange("p c h -> p (c h)")
    eac2 = eac.rearrange("p c h -> p (c h)")
    dtw2 = dtw.rearrange("p c h -> p (c h)")
    cdec2 = cdec.rearrange("p c h -> p (c h)")
    P.op("act", lambda e: e.activation(out=Aneg, in_=alog, func=AF.Exp), reads=[alog_b], writes=[Aneg_b])
    P.op("dve", lambda e: e.tensor_scalar(out=Aneg, in0=Aneg, scalar1=-1.0, scalar2=None, op0=ALU.mult), reads=[Aneg_b], writes=[Aneg_b])
    P.op("dve", lambda e: e.tensor_tensor(out=dts, in0=dts, in1=dtb[:, None, :].to_broadcast([128, nchunks, 4]), op=ALU.add),
         reads=[dts_b, dtb_b], writes=[dts_b])
    P.op("act", lambda e: e.activation(out=tm, in_=dts2, func=AF.Abs), reads=[dts_b], writes=[tm_b])
    P.op("act", lambda e: e.activation(out=te, in_=tm, func=AF.Exp, scale=-1.0), reads=[tm_b], writes=[te_b])
    P.op("act", lambda e: e.activation(out=te, in_=te, func=AF.Ln, bias=1.0), reads=[te_b], writes=[te_b])
    P.op("dve", lambda e: e.scalar_tensor_tensor(out=dts2, in0=dts2, scalar=0.0, in1=te, op0=ALU.max, op1=ALU.add),
         reads=[dts_b, te_b], writes=[dts_b])
    P.op("dve", lambda e: e.tensor_tensor(out=aal, in0=dts, in1=Aneg[:, None, :].to_broadcast([128, nchunks, 4]), op=ALU.mult),
         reads=[dts_b, Aneg_b], writes=[aal_b])
    pbig = [P.ps("pbig%d" % i, [128, 512]) for i in range(2)]
    for c0 in range(0, NC4, 512):
        n = min(512, NC4 - c0)
        pa, pab = pbig[0]
        pt_, ptb_ = pbig[1]
        P.op("pe", lambda e: e.matmul(pa[:, 0:n], lhsT=triu, rhs=aal2[:, c0:c0 + n], start=True, stop=True),
             reads=[triu_b, aal_b], writes=[pab])
        P.op("pe", lambda e: e.matmul(pt_[:, 0:n], lhsT=ones, rhs=aal2[:, c0:c0 + n], start=True, stop=True),
             reads=[ones_b, aal_b], writes=[ptb_])
        P.op("dve", lambda e: e.tensor_copy(out=acum2[:, c0:c0 + n], in_=pa[:, 0:n]), reads=[pab], writes=[acum_b])
        P.op("act", lambda e: e.activation(out=eac2[:, c0:c0 + n], in_=pa[:, 0:n], func=AF.Exp), reads=[pab], writes=[eac_b])
        P.op("act", lambda e: e.activation(out=cdec2[:, c0:c0 + n], in_=pt_[:, 0:n], func=AF.Exp), reads=[ptb_], writes=[cdec_b])
        P.op("dve", lambda e: e.tensor_tensor(out=tm[:, 0:n], in0=pt_[:, 0:n], in1=acum2[:, c0:c0 + n], op=ALU.subtract),
             reads=[ptb_, acum_b], writes=[tm_b])
        P.op("act", lambda e: e.activation(out=tm[:, 0:n], in_=tm[:, 0:n], func=AF.Exp), reads=[tm_b], writes=[tm_b])
        P.op("dve", lambda e: e.tensor_tensor(out=dtw2[:, c0:c0 + n], in0=tm[:, 0:n], in1=dts2[:, c0:c0 + n], op=ALU.mult),
             reads=[tm_b, dts_b], writes=[dtw_b])

    xc = [P.sb("xc%d" % i, [128, 4, 515], BF16) for i in range(2)]
    for t, b in xc:
        P.op("pool", lambda e: e.memset(t, 0.0), writes=[b])
    dgw, dgw_b = P.sb("dgw", [128, 4, 4, 128], BF16)
    for k in range(4):
        for j in range(4):
            P.op("pool", lambda e: e.tensor_scalar(out=dgw[:, k, j, :], in0=ident, scalar1=cw[:, k, j:j + 1], scalar2=None, op0=ALU.mult),
                 reads=[ident_b, cw_b], writes=[dgw_b])
    pcv, pcv_b = P.ps("pcv", [128, 512])
    szb = [P.sb("szb%d" % i, [128, 4, 256]) for i in range(2)]
    xsT = [P.sb("xsT%d" % i, [128, 4, 512], BF16) for i in range(2)]
    zt = [P.sb("zt%d" % i, [128, 4, 256], BF16) for i in range(2)]
    S32, S32_b = P.sb("S32", [128, 4, 64])
    Sbf, Sbf_b = P.sb("Sbf", [128, 256], BF16)
    P.op("pool", lambda e: e.memset(S32, 0.0), writes=[S32_b])
    P.op("pool", lambda e: e.memset(Sbf, 0.0), writes=[Sbf_b])
    xtok = [P.sb("xtok%d" % i, [128, 384], BF16) for i in range(2)]
    xdt = [P.sb("xdt%d" % i, [128, 256], BF16) for i in range(2)]
    xw = [P.sb("xw%d" % i, [128, 256], BF16) for i in range(2)]
    xd = [P.sb("xd%d" % i, [128, 256], BF16) for i in range(2)]
    arow = [P.sb("arow%d" % i, [128, 128]) for i in range(2)]
    Ee = [P.sb("Ee%d" % i, [128, 4, 128]) for i in range(2)]
    MT = [P.sb("MT%d" % i, [128, 4, 128], BF16) for i in range(2)]
    yo, yo_b = P.sb("yo", [128, 4, 64])
    y1, y1_b = P.sb("y1", [128, 256])
    sz, sz_b = P.sb("sz", [128, 256])
    sqj, sqj_b = P.sb("sqj", [128, 256])
    ssq, ssq_b = P.sb("ssq", [128, 1])
    yout = [P.sb("yout%d" % i, [128, 256], BF16) for i in range(2)]
    pTr, pTr_b = P.ps("pTr", [128, 3, 128], BF16)
    pD = pbig
    pCB, pCB_b = P.ps("pCB", [128, 128])
    pY, pY_b = P.ps("pY", [128, 512])
    pS, pS_b = P.ps("pS", [128, 256])
    pRow, pRow_b = P.ps("pRow", [128, 128])

    nbatch = nchunks // 4

    def batch_pro(bi):
        T0 = bi * 512
        xct, xcb = xc[bi % 2]
        if bi == 0:
            P.dma("sp", xct[:, :, 3:515], xbcT[:, 0:512].rearrange("(k p) t -> p k t", p=128), writes=[xcb])
        else:
            P.dma("sp", xct, xbcT[:, T0 - 3:T0 + 512].rearrange("(k p) t -> p k t", p=128), writes=[xcb])
        ztt, ztb = zt[bi % 2]
        P.dma("sp", ztt, z_d[T0:T0 + 512, :].rearrange("(c p) f -> p c f", p=128), writes=[ztb])
        xst, xsb = xsT[bi % 2]
        for k in range(4):
            for j in range(4):
                P.op("pe", lambda e: e.matmul(pcv, lhsT=dgw[:, k, j, :], rhs=xct[:, k, j:j + 512], start=(j == 0), stop=(j == 3)),
                     reads=[dgw_b, xcb], writes=[pcv_b], inc=(j == 3))
            P.op("act", lambda e: e.activation(out=xst[:, k, :], in_=pcv, func=AF.Silu, bias=cb[:, k, :]),
                 reads=[pcv_b, cb_b], writes=[xsb])
        szt, sztb = szb[bi % 2]
        P.op("act", lambda e: e.activation(out=szt, in_=ztt, func=AF.Silu), reads=[ztb], writes=[sztb])

    def front(c):
        bi, cc = c // 4, c % 4
        o = cc * 128
        par = c % 2
        xst, xsb = xsT[bi % 2]
        ztt, ztb = zt[bi % 2]
        xt_, xtb = xtok[par]
        x3 = xt_[:, 0:256].rearrange("p (h d) -> p h d", d=64)
        xdt_, xdtb = xdt[par]
        xw_, xwb = xw[par]
        xd_, xdb = xd[par]
        ar, arb = arow[par]
        pd, pdb = pD[par]
        ee, eeb = Ee[par]
        mt, mtb = MT[par]
        for i in range(3):
            P.op("pe", lambda e: e.transpose(out=pTr[:, i, :], in_=xst[:, i, o:o + 128], identity=ident),
                 reads=[xsb, ident_b], writes=[pTr_b], inc=(i == 2))
        xt_, xtb = xtok[par]
        P.op("act", lambda e: e.copy(out=xt_, in_=pTr.rearrange("p a b -> p (a b)")), reads=[pTr_b], writes=[xtb])
        x3 = xt_[:, 0:256].rearrange("p (h d) -> p h d", d=64)
        xdt_, xdtb = xdt[par]
        xw_, xwb = xw[par]
        xd_, xdb = xd[par]
        P.op("pool", lambda e: e.tensor_tensor(out=xdt_.rearrange("p (h d) -> p h d", d=64), in0=x3,
                                               in1=dts[:, c, :, None].to_broadcast([128, 4, 64]), op=ALU.mult),
             reads=[xtb, dts_b], writes=[xdtb])
        P.op("pool", lambda e: e.tensor_tensor(out=xw_.rearrange("p (h d) -> p h d", d=64), in0=x3,
                                               in1=dtw[:, c, :, None].to_broadcast([128, 4, 64]), op=ALU.mult),
             reads=[xtb, dtw_b], writes=[xwb])
        P.op("pool", lambda e: e.tensor_tensor(out=xd_.rearrange("p (h d) -> p h d", d=64), in0=x3,
                                               in1=dsk[:, :, None].to_broadcast([128, 4, 64]), op=ALU.mult),
             reads=[xtb, dsk_b], writes=[xdb])
        P.op("pe", lambda e: e.matmul(pRow, lhsT=aalp[:, c * 4:c * 4 + 128], rhs=triu, start=True, stop=True),
             reads=[aal_b, triu_b], writes=[pRow_b])
        ar, arb = arow[par]
        P.op("dve", lambda e: e.tensor_copy(out=ar, in_=pRow), reads=[pRow_b], writes=[arb])
        pd, pdb = pD[par]
        for e_ in range(4):
            osl = slice(e_ * 128, (e_ + 1) * 128)
            P.op("pe", lambda e: e.matmul(pd[:, osl], lhsT=sel[:, osl], rhs=ar, start=True, stop=False),
                 reads=[sel_b, arb], writes=[pdb], inc=False)
            P.op("pe", lambda e: e.matmul(pd[:, osl], lhsT=ar, rhs=nsel[:, osl], start=False, stop=False),
                 reads=[nsel_b, arb], writes=[pdb], inc=False)
            P.op("pe", lambda e: e.matmul(pd[:, osl], lhsT=ident, rhs=negm, start=False, stop=True),
                 reads=[ident_b, negm_b], writes=[pdb], inc=(e_ == 3))
        P.op("pe", lambda e: e.matmul(pCB, lhsT=xst[:, 2, o:o + 128], rhs=xst[:, 3, o:o + 128], start=True, stop=True),
             reads=[xsb], writes=[pCB_b])
        ee, eeb = Ee[par]
        P.op("act", lambda e: e.activation(out=ee.rearrange("p a b -> p (a b)"), in_=pd, func=AF.Exp), reads=[pdb], writes=[eeb])
        mt, mtb = MT[par]
        P.op("dve", lambda e: e.tensor_tensor(out=mt, in0=ee, in1=pCB[:, None, :].to_broadcast([128, 4, 128]), op=ALU.mult),
             reads=[eeb, pCB_b], writes=[mtb])

    def back(c):
        bi, cc = c // 4, c % 4
        o = cc * 128
        par = c % 2
        xst, xsb = xsT[bi % 2]
        ztt, ztb = zt[bi % 2]
        xt_, xtb = xtok[par]
        x3 = xt_[:, 0:256].rearrange("p (h d) -> p h d", d=64)
        xdt_, xdtb = xdt[par]
        xw_, xwb = xw[par]
        xd_, xdb = xd[par]
        ar, arb = arow[par]
        pd, pdb = pD[par]
        ee, eeb = Ee[par]
        mt, mtb = MT[par]
        for e_ in range(4):
            fs = slice(e_ * 64, (e_ + 1) * 64)
            P.op("pe", lambda e: e.matmul(pY[:, fs], lhsT=mt[:, e_, :], rhs=xdt_[:, fs], start=True, stop=False),
                 reads=[mtb, xdtb], writes=[pY_b], inc=False)
            P.op("pe", lambda e: e.matmul(pY[:, fs], lhsT=ident, rhs=xd_[:, fs], start=False, stop=True),
                 reads=[ident_b, xdb], writes=[pY_b], inc=False)
        P.op("pe", lambda e: e.matmul(pY[:, 256:512], lhsT=xst[:, 3, o:o + 128], rhs=Sbf, start=True, stop=True),
             reads=[xsb, Sbf_b], writes=[pY_b])
        P.op("pe", lambda e: e.matmul(pS, lhsT=xt_[:, 256:384], rhs=xw_, start=True, stop=True),
             reads=[xtb, xwb], writes=[pS_b])
        P.op("dve", lambda e: e.tensor_tensor(out=S32, in0=S32, in1=cdec[:, c, :, None].to_broadcast([128, 4, 64]), op=ALU.mult),
             reads=[S32_b, cdec_b], writes=[S32_b])
        P.op("dve", lambda e: e.tensor_tensor(out=S32.rearrange("p h d -> p (h d)"), in0=pS, in1=S32.rearrange("p h d -> p (h d)"), op=ALU.add),
             reads=[S32_b, pS_b], writes=[S32_b])
        P.op("act", lambda e: e.copy(out=Sbf, in_=S32.rearrange("p h d -> p (h d)")), reads=[S32_b], writes=[Sbf_b])
        P.op("dve", lambda e: e.tensor_tensor(out=yo, in0=pY[:, 256:512].rearrange("p (h d) -> p h d", d=64),
                                              in1=eac[:, c, :, None].to_broadcast([128, 4, 64]), op=ALU.mult),
             reads=[pY_b, eac_b], writes=[yo_b])
        P.op("dve", lambda e: e.tensor_tensor(out=y1, in0=pY[:, 0:256], in1=yo.rearrange("p h d -> p (h d)"), op=ALU.add),
             reads=[pY_b, yo_b], writes=[y1_b])
        szt, sztb = szb[bi % 2]
        P.op("pool", lambda e: e.tensor_tensor(out=y1, in0=y1, in1=szt[:, cc, :], op=ALU.mult), reads=[y1_b, sztb], writes=[y1_b])
        P.op("dve", lambda e: e.scalar_tensor_tensor(out=sqj, in0=y1, scalar=1.0 / 256, in1=y1, op0=ALU.mult, op1=ALU.mult, accum_out=ssq),
             reads=[y1_b], writes=[sqj_b, ssq_b])
        P.op("act", lambda e: e.activation(out=ssq, in_=ssq, func=AF.Ln, bias=1e-6), reads=[ssq_b], writes=[ssq_b])
        P.op("act", lambda e: e.activation(out=ssq, in_=ssq, func=AF.Exp, scale=-0.5), reads=[ssq_b], writes=[ssq_b])
        yt_, ytb = yout[par]
        P.op("dve", lambda e: e.scalar_tensor_tensor(out=yt_, in0=y1, scalar=ssq, in1=gn, op0=ALU.mult, op1=ALU.mult),
             reads=[y1_b, ssq_b, gn_b], writes=[ytb])
        P.dma("sp", y_o[c * 128:(c + 1) * 128, :], yt_, reads=[ytb])

    batch_pro(0)
    front(0)
    for c in range(nchunks):
        if c + 1 < nchunks:
            if (c + 1) % 4 == 0:
                batch_pro((c + 1) // 4)
            front(c + 1)
        back(c)
    P.finish()
    return nc
def build_ret(nchunks=NCH):
    nc = bass.Bass("TRN2", target_bir_lowering=False)
    P = Prog(nc)
    Lc = nchunks * 128

    def din(name, shape, dt=F32):
        return nc.dram_tensor(name, list(shape), dt, kind="ExternalInput").ap()

    qT_d = din("qT", [256, Lc], BF16)
    kT_d = din("kT", [256, Lc], BF16)
    v_d = din("v", [Lc, 256], BF16)
    lg_d = din("lg", [128, 1])
    dlt_d = din("dlt", [128, 128])
    rowl_d = din("rowl", [128, 128])
    colk_d = din("colk", [128, 1])
    ident_d = din("ident", [128, 128], BF16)
    o_o = nc.dram_tensor("o", [Lc, 256], BF16, kind="ExternalOutput").ap()

    def ld(name, shape, src, dt=F32):
        t, b = P.sb(name, shape, dt)
        P.dma("sp", t, src, writes=[b])
        return t, b

    ident, ident_b = ld("ident_s", [128, 128], ident_d, BF16)
    lg, lg_b = ld("lg_s", [128, 1], lg_d)
    dlt, dlt_b = ld("dlt_s", [128, 128], dlt_d)
    rowl, rowl_b = ld("rowl_s", [128, 128], rowl_d)
    colk, colk_b = ld("colk_s", [128, 1], colk_d)
    intraT, intraT_b = P.sb("intraT", [128, 128])
    gq, gq_b = P.sb("gq", [128, 128], BF16)
    kd, kd_b = P.sb("kdcol", [128, 1])
    cdec, cdec_b = P.sb("cdecr", [128, 1])
    P.op("act", lambda e: e.activation(out=intraT, in_=dlt, func=AF.Exp, scale=lg), reads=[dlt_b, lg_b], writes=[intraT_b])
    P.op("act", lambda e: e.activation(out=gq, in_=rowl, func=AF.Exp, scale=lg), reads=[rowl_b, lg_b], writes=[gq_b])
    P.op("act", lambda e: e.activation(out=kd, in_=colk, func=AF.Exp, scale=lg), reads=[colk_b, lg_b], writes=[kd_b])
    P.op("dve", lambda e: e.tensor_scalar(out=cdec, in0=lg, scalar1=128.0, scalar2=None, op0=ALU.mult), reads=[lg_b], writes=[cdec_b])
    P.op("act", lambda e: e.activation(out=cdec, in_=cdec, func=AF.Exp), reads=[cdec_b], writes=[cdec_b])

    qTb = [P.sb("qTb%d" % i, [128, 2, 512], BF16) for i in range(2)]
    kTb = [P.sb("kTb%d" % i, [128, 2, 512], BF16) for i in range(2)]
    vb = [P.sb("vb%d" % i, [128, 4, 256], BF16) for i in range(2)]
    S32, S32_b = P.sb("S32", [128, 512])
    Sbf, Sbf_b = P.sb("Sbf", [128, 2, 256], BF16)
    P.op("pool", lambda e: e.memset(S32, 0.0), writes=[S32_b])
    P.op("pool", lambda e: e.memset(Sbf, 0.0), writes=[Sbf_b])
    PT = [P.sb("PT%d" % i, [128, 128], BF16) for i in range(2)]
    kdec = [P.sb("kdec%d" % i, [128, 256], BF16) for i in range(2)]
    qdec = [P.sb("qdec%d" % i, [128, 2, 128], BF16) for i in range(2)]
    ot = [P.sb("ot%d" % i, [128, 256], BF16) for i in range(2)]
    pSc = [P.ps("pSc%d" % i, [128, 128]) for i in range(2)]
    pTr = [P.ps("pTr%d" % i, [128, 2, 128], BF16) for i in range(2)]
    pO = [P.ps("pOr%d" % i, [128, 256]) for i in range(2)]
    pS, pS_b = P.ps("pSr", [128, 512])

    for bi in range(nchunks // 4):
        T0 = bi * 512
        qt, qtb = qTb[bi % 2]
        kt, ktb = kTb[bi % 2]
        vt, vtb = vb[bi % 2]
        P.dma("sp", qt, qT_d[:, T0:T0 + 512].rearrange("(k p) t -> p k t", p=128), writes=[qtb])
        P.dma("sp", kt, kT_d[:, T0:T0 + 512].rearrange("(k p) t -> p k t", p=128), writes=[ktb])
        P.dma("sp", vt, v_d[T0:T0 + 512, :].rearrange("(c p) f -> p c f", p=128), writes=[vtb])
        for cc in range(4):
            c = bi * 4 + cc
            o = cc * 128
            par = c % 2
            psc, pscb = pSc[par]
            for k in range(2):
                P.op("pe", lambda e: e.matmul(psc, lhsT=kt[:, k, o:o + 128], rhs=qt[:, k, o:o + 128], start=(k == 0), stop=(k == 1)),
                     reads=[ktb, qtb], writes=[pscb], inc=(k == 1))
            ptr, ptrb = pTr[par]
            for k in range(2):
                P.op("pe", lambda e: e.transpose(out=ptr[:, k, :], in_=kt[:, k, o:o + 128], identity=ident),
                     reads=[ktb, ident_b], writes=[ptrb], inc=(k == 1))
            pt_, ptb_ = PT[par]
            P.op("dve", lambda e: e.tensor_tensor(out=pt_, in0=psc, in1=intraT, op=ALU.mult), reads=[pscb, intraT_b], writes=[ptb_])
            kdc, kdcb = kdec[par]
            P.op("act", lambda e: e.activation(out=kdc, in_=ptr.rearrange("p a b -> p (a b)"), func=AF.Copy, scale=kd),
                 reads=[ptrb, kd_b], writes=[kdcb])
            qd, qdb = qdec[par]
            P.op("pool", lambda e: e.tensor_tensor(out=qd, in0=qt[:, :, o:o + 128], in1=gq[:, None, :].to_broadcast([128, 2, 128]), op=ALU.mult),
                 reads=[qtb, gq_b], writes=[qdb])
            po, pob = pO[par]
            P.op("pe", lambda e: e.matmul(po, lhsT=pt_, rhs=vt[:, cc, :], start=True, stop=False), reads=[ptb_, vtb], writes=[pob], inc=False)
            for k in range(2):
                P.op("pe", lambda e: e.matmul(po, lhsT=qd[:, k, :], rhs=Sbf[:, k, :], start=False, stop=(k == 1)),
                     reads=[qdb, Sbf_b], writes=[pob], inc=(k == 1))
            for k in range(2):
                P.op("pe", lambda e: e.matmul(pS[:, k * 256:(k + 1) * 256], lhsT=kdc[:, k * 128:(k + 1) * 128], rhs=vt[:, cc, :], start=True, stop=True),
                     reads=[kdcb, vtb], writes=[pS_b], inc=(k == 1))
            P.op("dve", lambda e: e.scalar_tensor_tensor(out=S32, in0=S32, scalar=cdec, in1=pS, op0=ALU.mult, op1=ALU.add),
                 reads=[S32_b, cdec_b, pS_b], writes=[S32_b])
            P.op("act", lambda e: e.copy(out=Sbf.rearrange("p a b -> p (a b)"), in_=S32), reads=[S32_b], writes=[Sbf_b])
            ott, otb = ot[par]
            P.op("dve", lambda e: e.tensor_copy(out=ott, in_=po), reads=[pob], writes=[otb])
            P.dma("sp", o_o[c * 128:(c + 1) * 128, :], ott, reads=[otb])
    P.finish()
    return nc
TOPK = 256
NIT = 20
BIG = 1.0e30


def build_dsa(nblk=16, Lk=L_SEQ, stride=8):
    nc = bass.Bass("TRN2", target_bir_lowering=False)
    P = Prog(nc)

    def din(name, shape, dt=F32):
        return nc.dram_tensor(name, list(shape), dt, kind="ExternalInput").ap()

    q_d = din("q_l", [nblk, 128, 2048], BF16)
    qi_d = din("qi_l", [nblk, 128, 2048], BF16)
    wi_d = din("wi", [nblk * 128, 16])
    kT_d = din("kT", [256, Lk], BF16)
    kiT_d = din("kiT", [64, Lk], BF16)
    v_d = din("v", [Lk, 256], BF16)
    qpos_d = din("qpos", [128, nblk])
    kidx_d = din("kidx", [128, 1024])
    ident_d = din("ident", [128, 128], BF16)
    identf_d = din("identf", [128, 128])
    o_o = nc.dram_tensor("o", [nblk * 128, 1024], BF16, kind="ExternalOutput").ap()

    def ld(name, shape, src, dt=F32, q="sp"):
        t, b = P.sb(name, shape, dt)
        P.dma(q, t, src, writes=[b])
        return t, b

    ident, ident_b = ld("ident_s", [128, 128], ident_d, BF16)
    identf, identf_b = ld("identf_s", [128, 128], identf_d)
    oT, oT_b = P.sb("oT", [128, 512])
    kidx, kidx_b = ld("kidx_s", [128, 1024], kidx_d)
    qpos, qpos_b = ld("qpos_s", [128, nblk], qpos_d)
    kiTs, kiTs_b = P.sb("kiTs", [128, Lk], BF16)
    P.op("pool", lambda e: e.memset(kiTs[64:128, :], 0.0), writes=[kiTs_b])
    for c0 in range(0, Lk, 4096):
        c1 = min(Lk, c0 + 4096)
        P.dma("sp", kiTs[0:64, c0:c1], kiT_d[:, c0:c1], writes=[kiTs_b])
    kTb = [P.sb("kTb%d" % i, [128, 2, 128], BF16) for i in range(4)]
    scs = [P.sb("scores%d" % i, [128, Lk]) for i in range(2)]
    qg = [P.sb("qg%d" % i, [128, 4, 4, 128], BF16) for i in range(1)]
    qig = [P.sb("qig%d" % i, [128, 16, 128], BF16) for i in range(1)]
    wit, wit_b = P.sb("wit", [128, 16])
    wab, wab_b = P.sb("wab", [128, 16])
    wsg, wsg_b = P.sb("wsg", [128, 16])
    dg, dg_b = P.sb("dg", [128, 16, 128], BF16)
    rt = [P.sb("rt%d" % i, [128, 512], BF16) for i in range(3)]
    pen, pen_b = P.sb("pen", [128, 1024])
    tmpw, tmpw_b = P.sb("tmpw", [128, 1024])
    JCH = 4096
    junk, junk_b = P.sb("junk", [128, JCH], U8)
    cols = {}
    for nm in ("thc", "rmin", "rmin2", "rmax", "lo", "w", "mid", "cnt", "cnt2", "ge"):
        cols[nm] = P.sb("c_" + nm, [128, 1])
    vb = []
    for i in range(4):
        t, b = P.sb("vb%d" % i, [128, 4, 65], BF16)
        P.op("pool", lambda e: e.memset(t, 1.0), writes=[b])
        vb.append((t, b))
    selb = [P.sb("selb%d" % i, [128, 128], BF16) for i in range(2)]
    Et = [P.sb("Et%d" % i, [128, 4, 128], BF16) for i in range(3)]
    PTt = [P.sb("PTt%d" % i, [128, 4, 128], BF16) for i in range(3)]
    rden, rden_b = P.sb("rden", [128, 16])
    osb, osb_b = P.sb("osb", [128, 16, 64], BF16)
    pW = [P.ps("pW%d" % i, [128, 512]) for i in range(2)]
    pSc, pSc_b = P.ps("pScd", [128, 512])
    pSel = [P.ps("pSel0", [128, 128], BF16)]
    pOT = [P.ps("pOT%d" % i, [128, 512]) for i in range(4)]


    nW = [0]
    nR = [0]
    nV = [0]
    nE = [0]
    def phaseA(i):
        nk = stride * (i + 1)
        S = nk * 128
        sc, sc_b = scs[i % 2]
        qit, qib = qig[0]
        P.dma("sp", qit.rearrange("p a b -> p (a b)"), qi_d[i], writes=[qib])
        P.dma("sp", wit, wi_d[i * 128:(i + 1) * 128, :], writes=[wit_b])
        P.op("act", lambda e: e.activation(out=wab, in_=wit, func=AF.Abs), reads=[wit_b], writes=[wab_b])
        P.op("dve", lambda e: e.tensor_scalar(out=wsg, in0=wit, scalar1=0.0, scalar2=2.0, op0=ALU.is_gt, op1=ALU.mult),
             reads=[wit_b], writes=[wsg_b])
        P.op("dve", lambda e: e.tensor_scalar(out=wsg, in0=wsg, scalar1=-1.0, scalar2=None, op0=ALU.add), reads=[wsg_b], writes=[wsg_b])
        for h in range(16):
            P.op("pool", lambda e: e.tensor_scalar(out=dg[:, h, :], in0=ident, scalar1=wsg[:, h:h + 1], scalar2=None, op0=ALU.mult),
                 reads=[ident_b, wsg_b], writes=[dg_b])
        steps = [(kt, h) for kt in range(nk // 4) for h in range(16)]

        def back(info):
            kt, h, pw, pwb = info
            ks = slice(kt * 512, (kt + 1) * 512)
            r, rb = rt[nR[0] % 3]
            nR[0] += 1
            P.op("act", lambda e: e.activation(out=r, in_=pw, func=AF.Relu, scale=wab[:, h:h + 1]),
                 reads=[pwb, wab_b], writes=[rb])
            P.op("pe", lambda e: e.matmul(pSc, lhsT=dg[:, h, :], rhs=r, start=(h == 0), stop=(h == 15)),
                 reads=[dg_b, rb], writes=[pSc_b], inc=(h == 15))
            if h == 15:
                P.op("act", lambda e: e.copy(out=sc[:, ks], in_=pSc), reads=[pSc_b], writes=[sc_b])

        prev = None
        for kt, h in steps:
            ks = slice(kt * 512, (kt + 1) * 512)
            pw, pwb = pW[nW[0] % 2]
            nW[0] += 1
            P.op("pe", lambda e: e.matmul(pw, lhsT=qit[:, h, :], rhs=kiTs[:, ks], start=True, stop=True),
                 reads=[qib, kiTs_b], writes=[pwb])
            if prev is not None:
                back(prev)
            prev = (kt, h, pw, pwb)
        back(prev)
    def phaseB(i):
        nk = stride * (i + 1)
        S = nk * 128
        sc, sc_b = scs[i % 2]
        base = S - 1024
        thc, thc_b = cols["thc"]
        P.op("dve", lambda e: e.tensor_scalar(out=thc, in0=qpos[:, i:i + 1], scalar1=float(-base), scalar2=None, op0=ALU.add),
             reads=[qpos_b], writes=[thc_b])
        P.op("dve", lambda e: e.tensor_scalar(out=pen, in0=kidx, scalar1=thc, scalar2=-BIG, op0=ALU.is_gt, op1=ALU.mult),
             reads=[kidx_b, thc_b], writes=[pen_b])
        win = sc[:, base:S]
        P.op("dve", lambda e: e.tensor_tensor(out=tmpw, in0=win, in1=pen, op=ALU.subtract), reads=[sc_b, pen_b], writes=[tmpw_b])
        rmin, rmin_b = cols["rmin"]
        rmin2, rmin2_b = cols["rmin2"]
        rmax, rmax_b = cols["rmax"]
        P.op("dve", lambda e: e.tensor_reduce(out=rmin, in_=tmpw, op=ALU.min, axis=AX.X), reads=[tmpw_b], writes=[rmin_b])
        if base > 0:
            P.op("dve", lambda e: e.tensor_reduce(out=rmin2, in_=sc[:, 0:base], op=ALU.min, axis=AX.X), reads=[sc_b], writes=[rmin2_b])
            P.op("dve", lambda e: e.tensor_tensor(out=rmin, in0=rmin, in1=rmin2, op=ALU.min), reads=[rmin_b, rmin2_b], writes=[rmin_b])
        P.op("dve", lambda e: e.tensor_tensor(out=win, in0=win, in1=pen, op=ALU.add), reads=[sc_b, pen_b], writes=[sc_b])
        P.op("dve", lambda e: e.tensor_reduce(out=rmax, in_=sc[:, 0:S], op=ALU.max, axis=AX.X), reads=[sc_b], writes=[rmax_b])
        lo, lo_b = cols["lo"]
        w, w_b = cols["w"]
        mid, mid_b = cols["mid"]
        cnt, cnt_b = cols["cnt"]
        ge, ge_b = cols["ge"]
        P.op("dve", lambda e: e.tensor_copy(out=lo, in_=rmin), reads=[rmin_b], writes=[lo_b])
        P.op("dve", lambda e: e.tensor_tensor(out=w, in0=rmax, in1=rmin, op=ALU.subtract), reads=[rmax_b, rmin_b], writes=[w_b])
        for it in range(NIT):
            P.op("dve", lambda e: e.tensor_scalar(out=w, in0=w, scalar1=0.5, scalar2=None, op0=ALU.mult), reads=[w_b], writes=[w_b])
            P.op("dve", lambda e: e.tensor_tensor(out=mid, in0=lo, in1=w, op=ALU.add), reads=[lo_b, w_b], writes=[mid_b])
            cA, cA_b = cols["cnt"]
            cB, cB_b = cols["cnt2"]
            for j0 in range(0, S, JCH):
                j1 = min(S, j0 + JCH)
                if j0 == 0:
                    P.op("dve", lambda e: e.tensor_scalar(out=junk[:, 0:j1 - j0], in0=sc[:, j0:j1], scalar1=mid, scalar2=None,
                                                          op0=ALU.is_ge, op1=ALU.add, accum_out=cA),
                         reads=[sc_b, mid_b], writes=[junk_b, cA_b])
                else:
                    P.op("dve", lambda e: e.tensor_scalar(out=junk[:, 0:j1 - j0], in0=sc[:, j0:j1], scalar1=mid, scalar2=cB,
                                                          op0=ALU.is_ge, op1=ALU.add, accum_out=cA),
                         reads=[sc_b, mid_b, cB_b], writes=[junk_b, cA_b])
                cA, cA_b, cB, cB_b = cB, cB_b, cA, cA_b
            cnt, cnt_b = cB, cB_b
            P.op("dve", lambda e: e.tensor_scalar(out=ge, in0=cnt, scalar1=TOPK - 0.5, scalar2=None, op0=ALU.is_ge),
                 reads=[cnt_b], writes=[ge_b])
            P.op("dve", lambda e: e.scalar_tensor_tensor(out=lo, in0=w, scalar=ge, in1=lo, op0=ALU.mult, op1=ALU.add),
                 reads=[w_b, ge_b, lo_b], writes=[lo_b])
    def phaseD(i):
        nk = stride * (i + 1)
        S = nk * 128
        sc, sc_b = scs[i % 2]
        qgt, qgb = qg[0]
        lo, lo_b = cols['lo']
        P.dma("sp", qgt.rearrange("p a b c -> p (a b c)"), q_d[i], writes=[qgb])
        def backD(info):
            kb, g, pw, pwb, vt, vtb, psl, pslb = info
            et, etb = Et[nE[0] % 3]
            ptt, pttb = PTt[nE[0] % 3]
            nE[0] += 1
            P.op("act", lambda e: e.activation(out=et.rearrange("p a b -> p (a b)"), in_=pw, func=AF.Exp, scale=0.125),
                 reads=[pwb], writes=[etb])
            P.op("dve", lambda e: e.tensor_tensor(out=ptt, in0=et, in1=psl[:, None, :].to_broadcast([128, 4, 128]), op=ALU.mult),
                 reads=[etb, pslb], writes=[pttb])
            po, pob = pOT[g]
            P.op("pe", lambda e: e.matmul(po[0:65, :], lhsT=vt[:, g, :], rhs=ptt.rearrange("p a b -> p (a b)"),
                                          start=(kb == 0), stop=(kb == nk - 1)),
                 reads=[pttb, vtb], writes=[pob])

        prev = None
        for kb in range(nk):
            vt, vtb = vb[nV[0] % 4]
            ktt, kttb = kTb[nV[0] % 4]
            nV[0] += 1
            P.dma("sp", vt[:, :, 0:64], v_d[kb * 128:(kb + 1) * 128, :].rearrange("p (g d) -> p g d", d=64), writes=[vtb])
            P.dma("sp", ktt, kT_d[:, kb * 128:(kb + 1) * 128].rearrange("(k p) t -> p k t", p=128), writes=[kttb])
            for g in range(4):
                gl, gh = g % 2, g // 2
                pw, pwb = pW[nW[0] % 2]
                nW[0] += 1
                P.op("pe", lambda e: e.matmul(pw, lhsT=ktt[:, gh, :], rhs=qgt[:, g, :, :], start=True, stop=True),
                     reads=[kttb, qgb], writes=[pwb])
                if prev is not None:
                    backD(prev)
                if g == 0:
                    sb_, sbb = selb[kb % 2]
                    P.op("dve", lambda e: e.tensor_scalar(out=sb_, in0=sc[:, kb * 128:(kb + 1) * 128], scalar1=lo, scalar2=None, op0=ALU.is_ge),
                         reads=[sc_b, lo_b], writes=[sbb])
                    psl, pslb = pSel[0]
                    P.op("pe", lambda e: e.transpose(out=psl, in_=sb_, identity=ident), reads=[sbb, ident_b], writes=[pslb])
                prev = (kb, g, pw, pwb, vt, vtb, psl, pslb)
        backD(prev)
        for g in range(4):
            po, pob = pOT[g]
            P.op("act", lambda e: e.copy(out=oT[0:65, :], in_=po[0:65, :]), reads=[pob], writes=[oT_b])
            pw, pwb = pW[nW[0] % 2]
            nW[0] += 1
            for a in range(4):
                P.op("pe", lambda e: e.matmul(pw[:, a * 65:(a + 1) * 65], lhsT=oT[0:65, a * 128:(a + 1) * 128], rhs=identf[0:65, 0:65],
                                              start=True, stop=True),
                     reads=[oT_b, identf_b], writes=[pwb], inc=(a == 3))
            p3 = pw[:, 0:260].rearrange("p (h d) -> p h d", d=65)
            P.op("dve", lambda e: e.reciprocal(out=rden[:, g * 4:g * 4 + 4], in_=p3[:, :, 64]), reads=[pwb], writes=[rden_b])
            P.op("dve", lambda e: e.tensor_tensor(out=osb[:, g * 4:g * 4 + 4, :], in0=p3[:, :, 0:64],
                                                  in1=rden[:, g * 4:g * 4 + 4, None].to_broadcast([128, 4, 64]), op=ALU.mult),
                 reads=[pwb, rden_b], writes=[osb_b])
        P.dma("sp", o_o[i * 128:(i + 1) * 128, :], osb.rearrange("p h d -> p (h d)"), reads=[osb_b])

    phaseA(0)
    for i in range(nblk):
        if i + 1 < nblk:
            phaseA(i + 1)
        phaseB(i)
        phaseD(i)
    P.finish()
    return nc
import ml_dtypes as _mld

_BF = _mld.bfloat16
_PROGS = {}


def _prog(key, fn):
    if key not in _PROGS:
        _PROGS[key] = fn()
    return _PROGS[key]


def _run(nc, maps):
    res = run_bass_kernel_spmd(nc, maps, core_ids=list(range(8)))
    return res.results


def _c(a):
    return np.ascontiguousarray(a)


def _ssd_consts():
    k = np.arange(128)
    triu = (k[:, None] <= k[None, :]).astype(np.float32)
    negm = np.where(k[None, :] < k[:, None], -30000.0, 0.0).astype(_BF)
    sel = np.zeros((128, 4, 128), np.float32)
    for e in range(4):
        sel[e, e, :] = 1.0
    return dict(ident=np.eye(128, dtype=_BF), triu=triu, ones=np.ones((128, 128), np.float32), negm=negm,
                sel=sel.reshape(128, 512), nsel=(-sel).reshape(128, 512))


def _ssd_layer(z, xbcT, dt, conv_w, conv_b, dt_bias, a_log, d_skip, gnorm):
    C = _ssd_consts()
    maps = []
    for g in range(8):
        ch = np.concatenate([np.arange(g * 256, (g + 1) * 256), 2048 + np.arange(g * 128, (g + 1) * 128),
                             3072 + np.arange(g * 128, (g + 1) * 128)])
        m = dict(C)
        m.update(xbcT=_c(xbcT[ch]), z=_c(z[:, g * 256:(g + 1) * 256]),
                 dt=_c(dt[:, g * 4:(g + 1) * 4].reshape(NCH, 128, 4).transpose(1, 0, 2).reshape(128, NCH * 4)),
                 cw=_c(conv_w[:, ch].T.reshape(4, 128, 4).transpose(1, 0, 2).reshape(128, 16)),
                 cb=_c(conv_b[ch].reshape(4, 128).T),
                 dtb=_c(dt_bias[None, g * 4:(g + 1) * 4]), alog=_c(a_log[None, g * 4:(g + 1) * 4]),
                 dsk=_c(d_skip[None, g * 4:(g + 1) * 4]), gn=_c(gnorm[None, g * 256:(g + 1) * 256]))
        maps.append(m)
    res = _run(_prog("ssd", build_ssd), maps)
    return np.concatenate([r["y"] for r in res], axis=1)


def kernel(x, positions,
           l0_norm, l0_w_in, l0_conv_w, l0_conv_b, l0_dt_bias, l0_a_log, l0_d_skip, l0_gnorm, l0_w_out,
           l1_norm, l1_w_in, l1_gnorm, l1_w_out,
           l2_norm, l2_w_in, l2_idx_knorm, l2_w_out,
           l3_norm, l3_w_in, l3_conv_w, l3_conv_b, l3_dt_bias, l3_a_log, l3_d_skip, l3_gnorm, l3_w_out,
           final_norm):
    f32 = np.float32
    x = np.asarray(x, f32)[0]
    pos = np.asarray(positions)[0].astype(np.int32)
    L = x.shape[0]
    ident = np.eye(128, dtype=_BF)
    cont = [np.arange(c * NT, (c + 1) * NT) for c in range(8)]
    inter = [np.concatenate([np.arange((8 * i + c) * 128, (8 * i + c + 1) * 128) for i in range(16)]) for c in range(8)]

    maps = [dict(h_in=_c(x[cont[c]]), ident=ident, norm_g=_c(np.asarray(l0_norm, f32)[None, :]), w_in=_c(np.asarray(l0_w_in, f32)))
            for c in range(8)]
    r = _run(_prog("tok_none_ssd", lambda: build_tok(None, "ssd")), maps)
    z = np.concatenate([q["z"] for q in r], 0)
    xbcT = np.concatenate([q["xbcT"] for q in r], 1)
    dt = np.concatenate([q["dt"] for q in r], 0)
    y0 = _ssd_layer(z, xbcT, dt, np.asarray(l0_conv_w, f32), np.asarray(l0_conv_b, f32), np.asarray(l0_dt_bias, f32),
                    np.asarray(l0_a_log, f32), np.asarray(l0_d_skip, f32), np.asarray(l0_gnorm, f32))
    inv_ret = (1.0 / (np.float32(10000.0) ** np.linspace(0.0, 1.0, 128, dtype=np.float32))).astype(f32)[:, None]
    maps = [dict(h_in=_c(x[cont[c]]), a_in=_c(y0[cont[c]]), w_out=_c(np.asarray(l0_w_out, f32)), ident=ident,
                 norm_g=_c(np.asarray(l1_norm, f32)[None, :]), w_in=_c(np.asarray(l1_w_in, f32)),
                 pos=_c(pos[cont[c]][None, :]), inv=_c(inv_ret)) for c in range(8)]
    r = _run(_prog("tok_ssd_ret", lambda: build_tok("ssd", "ret")), maps)
    h1 = np.concatenate([q["h_out"] for q in r], 0)
    qT = np.concatenate([q["qT"] for q in r], 1)
    kT = np.concatenate([q["kT"] for q in r], 1)
    v1 = np.concatenate([q["v"] for q in r], 0)
    sg1 = np.concatenate([q["sg"] for q in r], 0)
    log_g = np.log1p(-np.exp2(-5.0 - np.arange(4, dtype=f32))).astype(f32)
    ii = np.arange(128)
    dlt = np.where(ii[None, :] >= ii[:, None], (ii[None, :] - ii[:, None]).astype(f32), 1e6).astype(f32)
    rowl = _c(np.broadcast_to((ii + 1).astype(f32)[None, :], (128, 128)))
    colk = _c((127 - ii).astype(f32)[:, None])
    maps = []
    for c in range(8):
        hh, half = c // 2, c % 2
        maps.append(dict(qT=_c(qT[hh * 256:(hh + 1) * 256]), kT=_c(kT[hh * 256:(hh + 1) * 256]),
                         v=_c(v1[:, hh * 512 + half * 256: hh * 512 + (half + 1) * 256]),
                         lg=np.full((128, 1), log_g[hh], f32), dlt=dlt, rowl=rowl, colk=colk, ident=ident))
    r = _run(_prog("ret", build_ret), maps)
    o1 = np.concatenate([q["o"] for q in r], 1)
    inv_dsa = (np.float32(500000.0) ** (-np.arange(0, 16, 2, dtype=f32) / 16)).astype(f32)[None, :]
    maps = [dict(h_in=_c(h1[inter[c]]), a_in=_c(o1[inter[c]]), sg_in=_c(sg1[inter[c]]), gn=_c(np.asarray(l1_gnorm, f32)[None, :]),
                 w_out=_c(np.asarray(l1_w_out, f32)), ident=ident, norm_g=_c(np.asarray(l2_norm, f32)[None, :]),
                 w_in=_c(np.asarray(l2_w_in, f32)), pos=_c(pos[inter[c]][:, None]), inv=_c(inv_dsa),
                 knorm=_c(np.asarray(l2_idx_knorm, f32)[None, :])) for c in range(8)]
    r2 = _run(_prog("tok_ret_dsa", lambda: build_tok("ret", "dsa")), maps)
    kT2 = np.zeros((256, L), _BF)
    kiT2 = np.zeros((64, L), _BF)
    v2 = np.zeros((L, 256), _BF)
    for c in range(8):
        kT2[:, inter[c]] = r2[c]["kT"]
        kiT2[:, inter[c]] = r2[c]["kiT"]
        v2[inter[c]] = r2[c]["v"]
    kidx = _c(np.broadcast_to(np.arange(1024, dtype=f32)[None, :], (128, 1024)))
    maps = []
    for c in range(8):
        qTc = r2[c]["qT"].reshape(4, 4, 64, 16, 128)
        q_l = np.zeros((16, 128, 4, 4, 128), _BF)
        for g in range(4):
            gl = g % 2
            q_l[:, gl * 64:(gl + 1) * 64, g] = qTc[g].transpose(2, 1, 0, 3)
        qiTc = r2[c]["qiT"].reshape(16, 64, 16, 128)
        qi_l = np.zeros((16, 128, 16, 128), _BF)
        qi_l[:, 0:64] = qiTc.transpose(2, 1, 0, 3)
        qpos = _c(inter[c].reshape(16, 128).T.astype(f32))
        maps.append(dict(q_l=q_l.reshape(16, 128, 2048), qi_l=qi_l.reshape(16, 128, 2048), wi=_c(r2[c]["wi"]),
                         kT=kT2, kiT=kiT2, v=v2, qpos=qpos, kidx=kidx, ident=ident, identf=np.eye(128, dtype=f32)))
    rd = _run(_prog("dsa", build_dsa), maps)
    maps = [dict(h_in=_c(r2[c]["h_out"]), a_in=_c(rd[c]["o"]), sg_in=_c(r2[c]["sg"]), w_out=_c(np.asarray(l2_w_out, f32)),
                 ident=ident, norm_g=_c(np.asarray(l3_norm, f32)[None, :]), w_in=_c(np.asarray(l3_w_in, f32))) for c in range(8)]
    r3 = _run(_prog("tok_dsa_ssd", lambda: build_tok("dsa", "ssd")), maps)
    z = np.zeros((L, 2048), _BF)
    xbcT = np.zeros((4096, L), _BF)
    dt = np.zeros((L, 32), f32)
    for c in range(8):
        z[inter[c]] = r3[c]["z"]
        xbcT[:, inter[c]] = r3[c]["xbcT"]
        dt[inter[c]] = r3[c]["dt"]
    y3 = _ssd_layer(z, xbcT, dt, np.asarray(l3_conv_w, f32), np.asarray(l3_conv_b, f32), np.asarray(l3_dt_bias, f32),
                    np.asarray(l3_a_log, f32), np.asarray(l3_d_skip, f32), np.asarray(l3_gnorm, f32))
    maps = [dict(h_in=_c(r3[c]["h_out"]), a_in=_c(y3[inter[c]]), w_out=_c(np.asarray(l3_w_out, f32)), ident=ident,
                 norm_g=_c(np.asarray(final_norm, f32)[None, :])) for c in range(8)]
    rf = _run(_prog("tok_ssd_final", lambda: build_tok("ssd", "final")), maps)
    out = np.zeros((1, L, D), f32)
    for c in range(8):
        out[0, inter[c]] = rf[c]["out"]
    global _DBG
    _DBG = dict(h1=h1, r2=r2, r3=r3, inter=inter)
    return out
```

```python
import numpy as np
import concourse.bass as bass
import concourse.mybir as mybir
from concourse.bass_utils import run_bass_kernel_spmd

F32 = mybir.dt.float32
BF16 = mybir.dt.bfloat16
I32 = mybir.dt.int32
U8 = mybir.dt.uint8
AF = mybir.ActivationFunctionType
ALU = mybir.AluOpType
AX = mybir.AxisListType

NDS = 24


class Buf:
    __slots__ = ("name", "lastw", "readers", "psum")

    def __init__(self, name, psum=False):
        self.name = name
        self.lastw = None
        self.readers = {}
        self.psum = psum


class Prog:
    def __init__(self, nc):
        self.nc = nc
        self.E = {"pe": nc.tensor, "dve": nc.vector, "act": nc.scalar,
                  "pool": nc.gpsimd, "sp": nc.sync}
        self.csem = {k: nc.alloc_semaphore("cs_" + k) for k in ("pe", "dve", "act", "pool")}
        self.ccnt = {k: 0 for k in self.csem}
        self.dsems = [nc.alloc_semaphore("ds%d" % i) for i in range(NDS)]
        self.dcnt = [0] * NDS
        self.dnext = 0
        self.xsems = []
        self.seen = {k: {} for k in self.E}
        self.pend_r = {k: [] for k in self.csem}
        self.pend_w = {k: [] for k in self.csem}
        self.nbuf = 0
        self.ninst = 0

    def sb(self, name, shape, dtype=F32):
        t = self.nc.alloc_sbuf_tensor(name, list(shape), dtype)
        return t.ap(), Buf(name)

    def ps(self, name, shape, dtype=F32):
        t = self.nc.alloc_psum_tensor(name, list(shape), dtype)
        return t.ap(), Buf(name, psum=True)

    def buf(self, name="b"):
        self.nbuf += 1
        return Buf("%s%d" % (name, self.nbuf))

    def _wait(self, eng, tok):
        kind, key, val = tok
        k = (kind, key)
        if self.seen[eng].get(k, 0) >= val:
            return
        sem = self.csem[key] if kind == "c" else (self.dsems[key] if kind == "d" else self.xsems[key])
        self.E[eng].wait_ge(sem, val)
        self.seen[eng][k] = val

    def _deps(self, eng, reads, writes):
        toks = []
        for b in reads:
            if b.lastw is not None:
                toks.append(b.lastw)
            if b.psum:
                for e2, t in b.readers.items():
                    if e2 != eng:
                        toks.append(t)
        for b in writes:
            if b.lastw is not None and not (b.lastw[0] == "c" and b.lastw[1] == eng):
                toks.append(b.lastw)
            for e2, t in b.readers.items():
                if e2 != eng or t[0] in ("d", "x"):
                    toks.append(t)
        for t in toks:
            self._wait(eng, t)

    def op(self, eng, fn, reads=(), writes=(), inc=True):
        for b in list(reads) + list(writes):
            for e2 in self.csem:
                if e2 != eng:
                    assert b not in self.pend_r[e2] and b not in self.pend_w[e2], \
                        "buffer %s used while pending on %s" % (b.name, e2)
        self._deps(eng, reads, writes)
        inst = fn(self.E[eng])
        self.ninst += 1
        if not inc:
            self.pend_r[eng].extend(reads)
            self.pend_w[eng].extend(writes)
            return inst
        self.ccnt[eng] += 1
        inst.then_inc(self.csem[eng], 1)
        tok = ("c", eng, self.ccnt[eng])
        for b in list(reads) + self.pend_r[eng]:
            b.readers[eng] = tok
        for b in list(writes) + self.pend_w[eng]:
            b.lastw = tok
            b.readers = {}
        self.pend_r[eng] = []
        self.pend_w[eng] = []
        return inst

    def dma(self, q, out, in_, reads=(), writes=(), **kw):
        for b in list(reads) + list(writes):
            for e2 in self.csem:
                assert b not in self.pend_r[e2] and b not in self.pend_w[e2]
        self._deps(q, reads, writes)
        if q == "pool":
            self.xsems.append(self.nc.alloc_semaphore("xs%d" % len(self.xsems)))
            inst = self.E[q].dma_start(out=out, in_=in_, **kw)
            self.ninst += 1
            inst.then_inc(self.xsems[-1], 16)
            tok = ("x", len(self.xsems) - 1, 16)
            for b in reads:
                b.readers[("q", q, "x%d" % len(self.xsems))] = tok
            for b in writes:
                b.lastw = tok
                b.readers = {}
            return tok
        s = self.dnext % NDS
        self.dnext += 1
        if self.dcnt[s] > 0:
            self._wait(q, ("d", s, self.dcnt[s]))
        inst = self.E[q].dma_start(out=out, in_=in_, **kw)
        self.ninst += 1
        self.dcnt[s] += 16
        inst.then_inc(self.dsems[s], 16)
        tok = ("d", s, self.dcnt[s])
        for b in reads:
            b.readers[("q", q, s)] = tok
        for b in writes:
            b.lastw = tok
            b.readers = {}
        return tok

    def finish(self):
        for s in range(NDS):
            if self.dcnt[s] > 0:
                self._wait("sp", ("d", s, self.dcnt[s]))
        for i in range(len(self.xsems)):
            self._wait("sp", ("x", i, 16))
        for e in self.csem:
            if self.ccnt[e] > 0:
                self._wait("sp", ("c", e, self.ccnt[e]))
NT = 2048
D = 1024
EPS = 1e-6
TWO_PI = 6.283185307179586
CW1 = 6.28125
CW2 = TWO_PI - CW1
MAGIC = 12582912.0
PI_SAFE = 3.1415925


def emit_sincos(P, r_in, rb, sin_out, sinb, cos_out, cosb, tmp, tmpb, tmp2, tmp2b):
    P.op("dve", lambda e: e.tensor_scalar(out=tmp, in0=r_in, scalar1=1.0 / TWO_PI, scalar2=MAGIC,
                                          op0=ALU.mult, op1=ALU.add), reads=[rb], writes=[tmpb])
    P.op("dve", lambda e: e.tensor_scalar(out=tmp, in0=tmp, scalar1=-MAGIC, scalar2=None, op0=ALU.add),
         reads=[tmpb], writes=[tmpb])
    P.op("dve", lambda e: e.scalar_tensor_tensor(out=tmp2, in0=tmp, scalar=-CW1, in1=r_in,
                                                 op0=ALU.mult, op1=ALU.add), reads=[tmpb, rb], writes=[tmp2b])
    P.op("dve", lambda e: e.scalar_tensor_tensor(out=r_in, in0=tmp, scalar=-CW2, in1=tmp2,
                                                 op0=ALU.mult, op1=ALU.add), reads=[tmpb, tmp2b], writes=[rb])
    P.op("dve", lambda e: e.tensor_scalar(out=r_in, in0=r_in, scalar1=-PI_SAFE, scalar2=PI_SAFE,
                                          op0=ALU.max, op1=ALU.min), reads=[rb], writes=[rb])
    P.op("act", lambda e: e.activation(out=sin_out, in_=r_in, func=AF.Sin), reads=[rb], writes=[sinb])
    P.op("dve", lambda e: e.tensor_scalar(out=tmp, in0=r_in, scalar1=1.5707963267948966, scalar2=-TWO_PI,
                                          op0=ALU.is_gt, op1=ALU.mult), reads=[rb], writes=[tmpb])
    P.op("dve", lambda e: e.tensor_tensor(out=tmp2, in0=tmp, in1=r_in, op=ALU.add), reads=[tmpb, rb], writes=[tmp2b])
    P.op("dve", lambda e: e.tensor_scalar(out=tmp2, in0=tmp2, scalar1=1.5707963267948966, scalar2=PI_SAFE,
                                          op0=ALU.add, op1=ALU.min), reads=[tmp2b], writes=[tmp2b])
    P.op("act", lambda e: e.activation(out=cos_out, in_=tmp2, func=AF.Sin), reads=[tmp2b], writes=[cosb])


def build_tok(act, proj):
    nc = bass.Bass("TRN2", target_bir_lowering=False)
    P = Prog(nc)

    def din(name, shape, dt=F32):
        return nc.dram_tensor(name, list(shape), dt, kind="ExternalInput").ap()

    def dout(name, shape, dt=F32):
        return nc.dram_tensor(name, list(shape), dt, kind="ExternalOutput").ap()

    Ka = {None: 0, "ssd": 2048, "ret": 2048, "dsa": 1024}[act]
    N = {"ssd": 6176, "ret": 6144, "dsa": 3664, "final": 0}[proj]
    h_in = din("h_in", [NT, D])
    ident_d = din("ident", [128, 128], BF16)
    norm_g = din("norm_g", [1, D])
    if act:
        a_in = din("a_in", [NT, Ka], BF16)
        w_out = din("w_out", [Ka, D])
        h_out = dout("h_out", [NT, D])
        if act in ("ret", "dsa"):
            sg_in = din("sg_in", [NT, Ka], BF16)
        if act == "ret":
            gn_d = din("gn", [1, Ka])
    if proj != "final":
        w_in = din("w_in", [D, N])
    if proj == "ssd":
        z_o = dout("z", [NT, 2048], BF16)
        xbcT_o = dout("xbcT", [4096, NT], BF16)
        dt_o = dout("dt", [NT, 32])
    elif proj == "ret":
        pos_d = din("pos", [1, NT], I32)
        inv_d = din("inv", [128, 1])
        qT_o = dout("qT", [1024, NT], BF16)
        kT_o = dout("kT", [1024, NT], BF16)
        v_o = dout("v", [NT, 2048], BF16)
        sg_o = dout("sg", [NT, 2048], BF16)
    elif proj == "dsa":
        pos_d = din("pos", [NT, 1], I32)
        inv_d = din("inv", [1, 8])
        knorm_d = din("knorm", [1, 64])
        qT_o = dout("qT", [1024, NT], BF16)
        kT_o = dout("kT", [256, NT], BF16)
        v_o = dout("v", [NT, 256], BF16)
        sg_o = dout("sg", [NT, 1024], BF16)
        qiT_o = dout("qiT", [1024, NT], BF16)
        kiT_o = dout("kiT", [64, NT], BF16)
        wi_o = dout("wi", [NT, 16])
    else:
        out_o = dout("out", [NT, D])

    ident, identb = P.sb("ident_sb", [128, 128], BF16)
    P.dma("sp", ident, ident_d, writes=[identb])
    gt, gtb = P.sb("gt", [128, D])
    P.dma("sp", gt, norm_g.partition_broadcast(128), writes=[gtb])
    if proj != "final":
        w_sb, w_b = P.sb("w_sb", [128, 8, N], BF16)
        w_bufs = [P.buf("w") for _ in range((N + 511) // 512)]
    if act:
        KA = Ka // 128
        wo_sb, wo_b = P.sb("wo_sb", [128, KA, D], BF16)
        wo_bufs = [P.buf("wo"), P.buf("wo")]
        for nb in range(2):
            P.dma("pool", wo_sb[:, :, nb * 512:(nb + 1) * 512],
                  w_out[:, nb * 512:(nb + 1) * 512].rearrange("(k p) n -> p k n", p=128), writes=[wo_bufs[nb]])
        a_t, a_b = P.sb("a_t", [128, Ka], BF16)
        aT, aT_b = P.sb("aT", [128, KA, 128], BF16)
        if act in ("ret", "dsa"):
            sg_t, sg_b = P.sb("sg_t", [128, Ka], BF16)
            ap_t, ap_b = P.sb("ap_t", [128, Ka], BF16)
        if act == "ret":
            gn_t, gn_b = P.sb("gn_t", [128, Ka])
            P.dma("sp", gn_t, gn_d.partition_broadcast(128), writes=[gn_b])
            ss4, ss4_b = P.sb("ss4", [128, 4])
            an_t, an_b = P.sb("an_t", [128, 512])
    if proj != "final":
        for c0 in range(0, N, 512):
            c1 = min(N, c0 + 512)
            P.dma("pool", w_sb[:, :, c0:c1], w_in[:, c0:c1].rearrange("(k p) n -> p k n", p=128), writes=[w_bufs[c0 // 512]])
    h_t, h_b = P.sb("h_t", [128, D])
    sq_t, sq_b = P.sb("sq_t", [128, D])
    ss, ss_b = P.sb("ss", [128, 1])
    if proj != "final":
        hn, hn_b = P.sb("hn", [128, D], BF16)
        hnT, hnT_b = P.sb("hnT", [128, 8, 512], BF16)
    NSTG = 3
    stg = [P.sb("stg%d" % i, [128, 512]) for i in range(NSTG)]
    stgh = [P.sb("stgh%d" % i, [128, 512], BF16) for i in range(NSTG)]
    stg_i = [0, 0]

    def next_stg(half):
        lst = stgh if half else stg
        i = stg_i[half] % NSTG
        stg_i[half] += 1
        return lst[i]

    pT = [P.ps("pT%d" % i, [128, 8, 128], BF16) for i in range(2)]
    pO, pO_b = P.ps("pO", [128, 1024])
    pP = [P.ps("pP%d" % i, [128, 512]) for i in range(4)]
    cnt = {"pT": 0, "pP": 0, "ev": 0}

    def next_pT():
        cnt["pT"] += 1
        return pT[cnt["pT"] % 2]

    def next_pP():
        cnt["pP"] += 1
        return pP[cnt["pP"] % 4]

    def evac_engine():
        cnt["ev"] += 1
        return "act" if cnt["ev"] % 2 else "dve"

    def copy(eng, out, in_, reads, writes):
        if eng == "act":
            P.op("act", lambda e: e.copy(out=out, in_=in_), reads=reads, writes=writes)
        else:
            P.op(eng, lambda e: e.tensor_copy(out=out, in_=in_), reads=reads, writes=writes)

    def transpose_blocks(src, src_b, nblk, dst, dst_b, dst_off=0):
        j = 0
        while j < nblk:
            n = min(8, nblk - j)
            pt, ptb = next_pT()
            for i in range(n):
                P.op("pe", lambda e: e.transpose(out=pt[:, i, :], in_=src[:, (j + i) * 128:(j + i + 1) * 128],
                                                 identity=ident),
                     reads=[src_b, identb], writes=[ptb], inc=(i == n - 1))
            copy(evac_engine(), dst[:, dst_off + j:dst_off + j + n, :], pt[:, 0:n, :], [ptb], [dst_b])
            j += n

    if proj == "ret":
        inv_c, inv_b = P.sb("inv_c", [128, 1])
        P.dma("sp", inv_c, inv_d, writes=[inv_b])
        posi, posi_b = P.sb("posi", [128, 512], I32)
        ang, ang_b = P.sb("ang", [128, 512])
        tA, tA_b = P.sb("tA", [128, 512])
        tB, tB_b = P.sb("tB", [128, 512])
        sinT, sinT_b = P.sb("sinT", [128, 512])
        cosT, cosT_b = P.sb("cosT", [128, 512])
        sinK, sinK_b = P.sb("sinK", [128, 512])
        cosK, cosK_b = P.sb("cosK", [128, 512])
    if proj == "dsa":
        inv8, inv8_b = P.sb("inv8", [128, 8])
        P.dma("sp", inv8, inv_d.partition_broadcast(128), writes=[inv8_b])
        kn_t, kn_b = P.sb("kn_t", [128, 64])
        P.dma("sp", kn_t, knorm_d.partition_broadcast(128), writes=[kn_b])
        posi, posi_b = P.sb("posi", [128, 1], I32)
        posf, posf_b = P.sb("posf", [128, 1])
        ang, ang_b = P.sb("ang", [128, 8])
        tA, tA_b = P.sb("tA", [128, 8])
        tB, tB_b = P.sb("tB", [128, 8])
        sin8, sin8_b = P.sb("sin8", [128, 8])
        cos8, cos8_b = P.sb("cos8", [128, 8])
        r1, r1_b = P.sb("r1", [128, 64])
        r2, r2_b = P.sb("r2", [128, 64])
        fT = {}
        for nm, nb in (("q", 8), ("k", 2), ("qi", 8), ("ki", 1)):
            fT[nm] = P.sb("fT_" + nm, [128, nb, 512], BF16)
        kis, kis_b = P.sb("kis", [128, 128], BF16)
        P.op("pool", lambda e: e.memset(kis, 0.0), writes=[kis_b])
        ki32, ki32_b = P.sb("ki32", [128, 64])
        wi32, wi32_b = P.sb("wi32", [128, 16])

    for st in range(NT // 512):
        for sub in range(4):
            t0 = st * 512 + sub * 128
            P.dma("sp", h_t, h_in[t0:t0 + 128, :], writes=[h_b])
            if act:
                P.dma("sp", a_t, a_in[t0:t0 + 128, :], writes=[a_b])
                src, src_b = a_t, a_b
                if act in ("ret", "dsa"):
                    P.dma("sp", sg_t, sg_in[t0:t0 + 128, :], writes=[sg_b])
                if act == "dsa":
                    P.op("dve", lambda e: e.tensor_tensor(out=ap_t, in0=a_t, in1=sg_t, op=ALU.mult),
                         reads=[a_b, sg_b], writes=[ap_b])
                    src, src_b = ap_t, ap_b
                if act == "ret":
                    for hh in range(4):
                        P.op("act", lambda e: e.activation(out=sq_t[:, 0:512], in_=a_t[:, hh * 512:(hh + 1) * 512],
                                                           func=AF.Square, accum_out=ss4[:, hh:hh + 1]),
                             reads=[a_b], writes=[sq_b, ss4_b])
                    P.op("act", lambda e: e.activation(out=ss4, in_=ss4, func=AF.Sqrt, scale=1.0 / 512, bias=EPS),
                         reads=[ss4_b], writes=[ss4_b])
                    P.op("dve", lambda e: e.reciprocal(out=ss4, in_=ss4), reads=[ss4_b], writes=[ss4_b])
                    for hh in range(4):
                        sl = slice(hh * 512, (hh + 1) * 512)
                        P.op("dve", lambda e: e.scalar_tensor_tensor(out=an_t, in0=a_t[:, sl], scalar=ss4[:, hh:hh + 1],
                                                                     in1=gn_t[:, sl], op0=ALU.mult, op1=ALU.mult),
                             reads=[a_b, ss4_b, gn_b], writes=[an_b])
                        P.op("dve", lambda e: e.tensor_tensor(out=ap_t[:, sl], in0=an_t, in1=sg_t[:, sl], op=ALU.mult),
                             reads=[an_b, sg_b], writes=[ap_b])
                    src, src_b = ap_t, ap_b
                transpose_blocks(src, src_b, KA, aT, aT_b)
                for nb in range(2):
                    for k in range(KA):
                        P.op("pe", lambda e: e.matmul(pO[:, nb * 512:(nb + 1) * 512], lhsT=aT[:, k, :],
                                                      rhs=wo_sb[:, k, nb * 512:(nb + 1) * 512],
                                                      start=(k == 0), stop=(k == KA - 1)),
                             reads=[aT_b, wo_bufs[nb]], writes=[pO_b], inc=(nb == 1 and k == KA - 1))
                P.op("dve", lambda e: e.tensor_tensor(out=h_t, in0=pO, in1=h_t, op=ALU.add),
                     reads=[pO_b, h_b], writes=[h_b])
                P.dma("sp", h_out[t0:t0 + 128, :], h_t, reads=[h_b])
            P.op("act", lambda e: e.activation(out=sq_t, in_=h_t, func=AF.Square, accum_out=ss),
                 reads=[h_b], writes=[sq_b, ss_b])
            P.op("act", lambda e: e.activation(out=ss, in_=ss, func=AF.Sqrt, scale=1.0 / D, bias=EPS),
                 reads=[ss_b], writes=[ss_b])
            P.op("dve", lambda e: e.reciprocal(out=ss, in_=ss), reads=[ss_b], writes=[ss_b])
            if proj == "final":
                P.op("dve", lambda e: e.scalar_tensor_tensor(out=sq_t, in0=h_t, scalar=ss, in1=gt,
                                                             op0=ALU.mult, op1=ALU.mult),
                     reads=[h_b, ss_b, gtb], writes=[sq_b])
                P.dma("sp", out_o[t0:t0 + 128, :], sq_t, reads=[sq_b])
                continue
            P.op("dve", lambda e: e.scalar_tensor_tensor(out=hn, in0=h_t, scalar=ss, in1=gt,
                                                         op0=ALU.mult, op1=ALU.mult),
                 reads=[h_b, ss_b, gtb], writes=[hn_b])
            pt, ptb = next_pT()
            for k in range(8):
                P.op("pe", lambda e: e.transpose(out=pt[:, k, :], in_=hn[:, k * 128:(k + 1) * 128], identity=ident),
                     reads=[hn_b, identb], writes=[ptb], inc=(k == 7))
            copy(evac_engine(), hnT[:, :, sub * 128:(sub + 1) * 128], pt, [ptb], [hnT_b])
        if proj == "final":
            continue
        T0 = st * 512

        def mm_tok(sub, c0, ncol):
            pp, ppb = next_pP()
            for k in range(8):
                P.op("pe", lambda e: e.matmul(pp[:, 0:ncol], lhsT=hnT[:, k, sub * 128:(sub + 1) * 128],
                                              rhs=w_sb[:, k, c0:c0 + ncol], start=(k == 0), stop=(k == 7)),
                     reads=[hnT_b, w_bufs[c0 // 512]], writes=[ppb], inc=(k == 7))
            return pp, ppb

        def mm_feat(f0):
            pp, ppb = next_pP()
            for k in range(8):
                P.op("pe", lambda e: e.matmul(pp, lhsT=w_sb[:, k, f0:f0 + 128], rhs=hnT[:, k, :],
                                              start=(k == 0), stop=(k == 7)),
                     reads=[hnT_b, w_bufs[f0 // 512]], writes=[ppb], inc=(k == 7))
            return pp, ppb

        def tok_block_out(c0, ncol, dst, dcol, func=None, fp32=False):
            for sub in range(4):
                pp, ppb = mm_tok(sub, c0, ncol)
                s, sb_ = next_stg(0 if fp32 else 1)
                if func is None:
                    copy(evac_engine(), s[:, 0:ncol], pp[:, 0:ncol], [ppb], [sb_])
                else:
                    P.op("act", lambda e: e.activation(out=s[:, 0:ncol], in_=pp[:, 0:ncol], func=func),
                         reads=[ppb], writes=[sb_])
                r0 = T0 + sub * 128
                P.dma("sp", dst[r0:r0 + 128, dcol:dcol + ncol], s[:, 0:ncol], reads=[sb_])

        if proj == "ssd":
            for blk in range(4):
                tok_block_out(blk * 512, 512, z_o, blk * 512)
            for fb in range(32):
                pp, ppb = mm_feat(2048 + fb * 128)
                s, sb_ = next_stg(1)
                copy(evac_engine(), s, pp, [ppb], [sb_])
                P.dma("sp", xbcT_o[fb * 128:(fb + 1) * 128, T0:T0 + 512], s, reads=[sb_])
            tok_block_out(6144, 32, dt_o, 0, fp32=True)
        elif proj == "ret":
            P.dma("sp", posi, pos_d[:, T0:T0 + 512].partition_broadcast(128), writes=[posi_b])
            P.op("dve", lambda e: e.tensor_copy(out=ang, in_=posi), reads=[posi_b], writes=[ang_b])
            P.op("dve", lambda e: e.tensor_scalar(out=ang, in0=ang, scalar1=inv_c, scalar2=None, op0=ALU.mult),
                 reads=[ang_b, inv_b], writes=[ang_b])
            emit_sincos(P, ang, ang_b, sinT, sinT_b, cosT, cosT_b, tA, tA_b, tB, tB_b)
            P.op("pool", lambda e: e.tensor_scalar(out=sinK, in0=sinT, scalar1=0.0625, scalar2=None, op0=ALU.mult),
                 reads=[sinT_b], writes=[sinK_b])
            P.op("pool", lambda e: e.tensor_scalar(out=cosK, in0=cosT, scalar1=0.0625, scalar2=None, op0=ALU.mult),
                 reads=[cosT_b], writes=[cosK_b])
            for which, base, dst in (("q", 0, qT_o), ("k", 1024, kT_o)):
                cs, cs_b, sn, sn_b = (cosT, cosT_b, sinT, sinT_b) if which == "q" else (cosK, cosK_b, sinK, sinK_b)
                for hh in range(4):
                    p1, p1b = mm_feat(base + hh * 256)
                    p2, p2b = mm_feat(base + hh * 256 + 128)
                    o1, o1b = next_stg(1)
                    o2, o2b = next_stg(1)
                    P.op("dve", lambda e: e.tensor_tensor(out=tA, in0=p1, in1=cs, op=ALU.mult), reads=[p1b, cs_b], writes=[tA_b])
                    P.op("dve", lambda e: e.tensor_tensor(out=tB, in0=p2, in1=sn, op=ALU.mult), reads=[p2b, sn_b], writes=[tB_b])
                    P.op("pool", lambda e: e.tensor_tensor(out=o1, in0=tA, in1=tB, op=ALU.subtract), reads=[tA_b, tB_b], writes=[o1b])
                    P.op("dve", lambda e: e.tensor_tensor(out=ang, in0=p2, in1=cs, op=ALU.mult), reads=[p2b, cs_b], writes=[ang_b])
                    P.op("dve", lambda e: e.tensor_tensor(out=sq_t[:, 0:512], in0=p1, in1=sn, op=ALU.mult), reads=[p1b, sn_b], writes=[sq_b])
                    P.op("pool", lambda e: e.tensor_tensor(out=o2, in0=ang, in1=sq_t[:, 0:512], op=ALU.add), reads=[ang_b, sq_b], writes=[o2b])
                    f0 = hh * 256
                    P.dma("sp", dst[f0:f0 + 128, T0:T0 + 512], o1, reads=[o1b])
                    P.dma("sp", dst[f0 + 128:f0 + 256, T0:T0 + 512], o2, reads=[o2b])
            for blk in range(4):
                tok_block_out(2048 + blk * 512, 512, v_o, blk * 512)
            for blk in range(4):
                tok_block_out(4096 + blk * 512, 512, sg_o, blk * 512, func=AF.Silu)
        elif proj == "dsa":
            for sub in range(4):
                r0 = T0 + sub * 128
                P.dma("sp", posi, pos_d[r0:r0 + 128, :], writes=[posi_b])
                P.op("dve", lambda e: e.tensor_copy(out=posf, in_=posi), reads=[posi_b], writes=[posf_b])
                P.op("dve", lambda e: e.tensor_scalar(out=ang, in0=inv8, scalar1=posf, scalar2=None, op0=ALU.mult),
                     reads=[inv8_b, posf_b], writes=[ang_b])
                emit_sincos(P, ang, ang_b, sin8, sin8_b, cos8, cos8_b, tA, tA_b, tB, tB_b)

                def rope_tok(pp, ppb, nh, s, sb_):
                    p3 = pp[:, 0:nh * 64].rearrange("p (h d) -> p h d", d=64)
                    s3 = s[:, 0:nh * 64].rearrange("p (h d) -> p h d", d=64)
                    cb = cos8[:, None, :].to_broadcast([128, nh, 8])
                    sb2 = sin8[:, None, :].to_broadcast([128, nh, 8])
                    a3 = r1[:, 0:nh * 8].rearrange("p (h d) -> p h d", d=8)
                    b3 = r2[:, 0:nh * 8].rearrange("p (h d) -> p h d", d=8)
                    P.op("act", lambda e: e.copy(out=s[:, 0:nh * 64], in_=pp[:, 0:nh * 64]), reads=[ppb], writes=[sb_])
                    P.op("dve", lambda e: e.tensor_tensor(out=a3, in0=p3[:, :, 0:8], in1=cb, op=ALU.mult), reads=[ppb, cos8_b], writes=[r1_b])
                    P.op("dve", lambda e: e.tensor_tensor(out=b3, in0=p3[:, :, 8:16], in1=sb2, op=ALU.mult), reads=[ppb, sin8_b], writes=[r2_b])
                    P.op("dve", lambda e: e.tensor_tensor(out=s3[:, :, 0:8], in0=a3, in1=b3, op=ALU.subtract), reads=[r1_b, r2_b], writes=[sb_])
                    P.op("dve", lambda e: e.tensor_tensor(out=a3, in0=p3[:, :, 8:16], in1=cb, op=ALU.mult), reads=[ppb, cos8_b], writes=[r1_b])
                    P.op("dve", lambda e: e.tensor_tensor(out=b3, in0=p3[:, :, 0:8], in1=sb2, op=ALU.mult), reads=[ppb, sin8_b], writes=[r2_b])
                    P.op("dve", lambda e: e.tensor_tensor(out=s3[:, :, 8:16], in0=a3, in1=b3, op=ALU.add), reads=[r1_b, r2_b], writes=[sb_])

                for nm, cbase in (("q", 0), ("qi", 2560)):
                    ft, ftb = fT[nm]
                    for blk in range(2):
                        pp, ppb = mm_tok(sub, cbase + blk * 512, 512)
                        s, sb_ = next_stg(1)
                        rope_tok(pp, ppb, 8, s, sb_)
                        pt, ptb = next_pT()
                        for i in range(4):
                            P.op("pe", lambda e: e.transpose(out=pt[:, i, :], in_=s[:, i * 128:(i + 1) * 128], identity=ident),
                                 reads=[sb_, identb], writes=[ptb], inc=(i == 3))
                        copy(evac_engine(), ft[:, blk * 4:blk * 4 + 4, sub * 128:(sub + 1) * 128], pt[:, 0:4, :], [ptb], [ftb])
                pp, ppb = mm_tok(sub, 1024, 512)
                s, sb_ = next_stg(1)
                rope_tok(pp, ppb, 4, s, sb_)
                ft, ftb = fT["k"]
                pt, ptb = next_pT()
                for i in range(2):
                    P.op("pe", lambda e: e.transpose(out=pt[:, i, :], in_=s[:, i * 128:(i + 1) * 128], identity=ident),
                         reads=[sb_, identb], writes=[ptb], inc=(i == 1))
                copy(evac_engine(), ft[:, 0:2, sub * 128:(sub + 1) * 128], pt[:, 0:2, :], [ptb], [ftb])
                s2, s2b = next_stg(1)
                copy(evac_engine(), s2[:, 0:256], pp[:, 256:512], [ppb], [s2b])
                P.dma("sp", v_o[r0:r0 + 128, :], s2[:, 0:256], reads=[s2b])
                pp, ppb = mm_tok(sub, 3584, 80)
                P.op("dve", lambda e: e.tensor_scalar(out=wi32, in0=pp[:, 64:80], scalar1=1.0 / 32, scalar2=None, op0=ALU.mult),
                     reads=[ppb], writes=[wi32_b])
                P.dma("sp", wi_o[r0:r0 + 128, :], wi32, reads=[wi32_b])
                P.op("act", lambda e: e.activation(out=sq_t[:, 0:64], in_=pp[:, 0:64], func=AF.Square, accum_out=ss),
                     reads=[ppb], writes=[sq_b, ss_b])
                P.op("act", lambda e: e.activation(out=ss, in_=ss, func=AF.Sqrt, scale=1.0 / 64, bias=EPS), reads=[ss_b], writes=[ss_b])
                P.op("dve", lambda e: e.reciprocal(out=ss, in_=ss), reads=[ss_b], writes=[ss_b])
                P.op("dve", lambda e: e.scalar_tensor_tensor(out=ki32, in0=pp[:, 0:64], scalar=ss, in1=kn_t, op0=ALU.mult, op1=ALU.mult),
                     reads=[ppb, ss_b, kn_b], writes=[ki32_b])
                rope_tok(ki32, ki32_b, 1, kis, kis_b)
                ft, ftb = fT["ki"]
                pt, ptb = next_pT()
                P.op("pe", lambda e: e.transpose(out=pt[:, 0, :], in_=kis, identity=ident), reads=[kis_b, identb], writes=[ptb])
                copy(evac_engine(), ft[:, 0:1, sub * 128:(sub + 1) * 128], pt[:, 0:1, :], [ptb], [ftb])
            for nm, dst, nb in (("q", qT_o, 8), ("qi", qiT_o, 8), ("k", kT_o, 2)):
                ft, ftb = fT[nm]
                P.dma("sp", dst[:, T0:T0 + 512].rearrange("(b p) t -> p b t", p=128), ft, reads=[ftb])
            ft, ftb = fT["ki"]
            P.dma("sp", kiT_o[:, T0:T0 + 512], ft[0:64, 0, :], reads=[ftb])
            for blk in range(2):
                tok_block_out(1536 + blk * 512, 512, sg_o, blk * 512, func=AF.Silu)
    P.finish()
    return nc
L_SEQ = 16384
NCH = L_SEQ // 128


def build_ssd(nchunks=NCH):
    nc = bass.Bass("TRN2", target_bir_lowering=False)
    P = Prog(nc)
    Lc = nchunks * 128

    def din(name, shape, dt=F32):
        return nc.dram_tensor(name, list(shape), dt, kind="ExternalInput").ap()

    xbcT = din("xbcT", [512, Lc], BF16)
    z_d = din("z", [Lc, 256], BF16)
    dt_d = din("dt", [128, nchunks * 4])
    cw_d = din("cw", [128, 16])
    cb_d = din("cb", [128, 4])
    dtb_d = din("dtb", [1, 4])
    alog_d = din("alog", [1, 4])
    dsk_d = din("dsk", [1, 4])
    gn_d = din("gn", [1, 256])
    ident_d = din("ident", [128, 128], BF16)
    triu_d = din("triu", [128, 128])
    ones_d = din("ones", [128, 128])
    negm_d = din("negm", [128, 128], BF16)
    sel_d = din("sel", [128, 4 * 128])
    nsel_d = din("nsel", [128, 4 * 128])
    y_o = nc.dram_tensor("y", [Lc, 256], BF16, kind="ExternalOutput").ap()

    def ld(name, shape, src, dt=F32):
        t, b = P.sb(name, shape, dt)
        P.dma("sp", t, src, writes=[b])
        return t, b

    ident, ident_b = ld("ident_s", [128, 128], ident_d, BF16)
    triu, triu_b = ld("triu_s", [128, 128], triu_d)
    ones, ones_b = ld("ones_s", [128, 128], ones_d)
    negm, negm_b = ld("negm_s", [128, 128], negm_d, BF16)
    sel, sel_b = ld("sel_s", [128, 512], sel_d)
    nsel, nsel_b = ld("nsel_s", [128, 512], nsel_d)
    cw2, cw_b = ld("cw_s", [128, 16], cw_d)
    cw = cw2.rearrange("p (k j) -> p k j", j=4)
    cb2, cb_b = ld("cb_s", [128, 4], cb_d)
    cb = cb2.rearrange("p (k j) -> p k j", j=1)
    dtb, dtb_b = ld("dtb_s", [128, 4], dtb_d.partition_broadcast(128))
    alog, alog_b = ld("alog_s", [128, 4], alog_d.partition_broadcast(128))
    dsk, dsk_b = ld("dsk_s", [128, 4], dsk_d.partition_broadcast(128))
    gn, gn_b = ld("gn_s", [128, 256], gn_d.partition_broadcast(128))
    NC4 = nchunks * 4
    dts2, dts_b = ld("dts", [128, nchunks * 4], dt_d)
    dts = dts2.rearrange("p (c h) -> p c h", h=4)

    tm, tm_b = P.sb("tm", [128, NC4])
    te, te_b = P.sb("te", [128, NC4])
    aalp, aal_b = P.sb("aal", [128, NC4 + 128])
    P.op("pool", lambda e: e.memset(aalp, 0.0), writes=[aal_b])
    aal = aalp[:, 0:NC4].rearrange("p (c h) -> p c h", h=4)
    acum, acum_b = P.sb("acum", [128, nchunks, 4])
    eac, eac_b = P.sb("eac", [128, nchunks, 4])
    dtw, dtw_b = P.sb("dtw", [128, nchunks, 4])
    cdec, cdec_b = P.sb("cdec", [128, nchunks, 4])
    Aneg, Aneg_b = P.sb("Aneg", [128, 4])
    aal2 = aalp
    acum2 = acum.rearrange("p c h -> p (c h)")
    eac2 = eac.rearrange("p c h -> p (c h)")
    dtw2 = dtw.rearrange("p c h -> p (c h)")
    cdec2 = cdec.rearrange("p c h -> p (c h)")
    P.op("act", lambda e: e.activation(out=Aneg, in_=alog, func=AF.Exp), reads=[alog_b], writes=[Aneg_b])
    P.op("dve", lambda e: e.tensor_scalar(out=Aneg, in0=Aneg, scalar1=-1.0, scalar2=None, op0=ALU.mult), reads=[Aneg_b], writes=[Aneg_b])
    P.op("dve", lambda e: e.tensor_tensor(out=dts, in0=dts, in1=dtb[:, None, :].to_broadcast([128, nchunks, 4]), op=ALU.add),
         reads=[dts_b, dtb_b], writes=[dts_b])
    P.op("act", lambda e: e.activation(out=tm, in_=dts2, func=AF.Abs), reads=[dts_b], writes=[tm_b])
    P.op("act", lambda e: e.activation(out=te, in_=tm, func=AF.Exp, scale=-1.0), reads=[tm_b], writes=[te_b])
    P.op("act", lambda e: e.activation(out=te, in_=te, func=AF.Ln, bias=1.0), reads=[te_b], writes=[te_b])
    P.op("dve", lambda e: e.scalar_tensor_tensor(out=dts2, in0=dts2, scalar=0.0, in1=te, op0=ALU.max, op1=ALU.add),
         reads=[dts_b, te_b], writes=[dts_b])
    P.op("dve", lambda e: e.tensor_tensor(out=aal, in0=dts, in1=Aneg[:, None, :].to_broadcast([128, nchunks, 4]), op=ALU.mult),
         reads=[dts_b, Aneg_b], writes=[aal_b])
    pbig = [P.ps("pbig%d" % i, [128, 512]) for i in range(2)]
    for c0 in range(0, NC4, 512):
        n = min(512, NC4 - c0)
        pa, pab = pbig[0]
        pt_, ptb_ = pbig[1]
        P.op("pe", lambda e: e.matmul(pa[:, 0:n], lhsT=triu, rhs=aal2[:, c0:c0 + n], start=True, stop=True),
             reads=[triu_b, aal_b], writes=[pab])
        P.op("pe", lambda e: e.matmul(pt_[:, 0:n], lhsT=ones, rhs=aal2[:, c0:c0 + n], start=True, stop=True),
             reads=[ones_b, aal_b], writes=[ptb_])
        P.op("dve", lambda e: e.tensor_copy(out=acum2[:, c0:c0 + n], in_=pa[:, 0:n]), reads=[pab], writes=[acum_b])
        P.op("act", lambda e: e.activation(out=eac2[:, c0:c0 + n], in_=pa[:, 0:n], func=AF.Exp), reads=[pab], writes=[eac_b])
        P.op("act", lambda e: e.activation(out=cdec2[:, c0:c0 + n], in_=pt_[:, 0:n], func=AF.Exp), reads=[ptb_], writes=[cdec_b])
        P.op("dve", lambda e: e.tensor_tensor(out=tm[:, 0:n], in0=pt_[:, 0:n], in1=acum2[:, c0:c0 + n], op=ALU.subtract),
             reads=[ptb_, acum_b], writes=[tm_b])
        P.op("act", lambda e: e.activation(out=tm[:, 0:n], in_=tm[:, 0:n], func=AF.Exp), reads=[tm_b], writes=[tm_b])
        P.op("dve", lambda e: e.tensor_tensor(out=dtw2[:, c0:c0 + n], in0=tm[:, 0:n], in1=dts2[:, c0:c0 + n], op=ALU.mult),
             reads=[tm_b, dts_b], writes=[dtw_b])

    xc = [P.sb("xc%d" % i, [128, 4, 515], BF16) for i in range(2)]
    for t, b in xc:
        P.op("pool", lambda e: e.memset(t, 0.0), writes=[b])
    dgw, dgw_b = P.sb("dgw", [128, 4, 4, 128], BF16)
    for k in range(4):
        for j in range(4):
            P.op("pool", lambda e: e.tensor_scalar(out=dgw[:, k, j, :], in0=ident, scalar1=cw[:, k, j:j + 1], scalar2=None, op0=ALU.mult),
                 reads=[ident_b, cw_b], writes=[dgw_b])
    pcv, pcv_b = P.ps("pcv", [128, 512])
    szb = [P.sb("szb%d" % i, [128, 4, 256]) for i in range(2)]
    xsT = [P.sb("xsT%d" % i, [128, 4, 512], BF16) for i in range(2)]
    zt = [P.sb("zt%d" % i, [128, 4, 256], BF16) for i in range(2)]
    S32, S32_b = P.sb("S32", [128, 4, 64])
    Sbf, Sbf_b = P.sb("Sbf", [128, 256], BF16)
    P.op("pool", lambda e: e.memset(S32, 0.0), writes=[S32_b])
    P.op("pool", lambda e: e.memset(Sbf, 0.0), writes=[Sbf_b])
    xtok = [P.sb("xtok%d" % i, [128, 384], BF16) for i in range(2)]
    xdt = [P.sb("xdt%d" % i, [128, 256], BF16) for i in range(2)]
    xw = [P.sb("xw%d" % i, [128, 256], BF16) for i in range(2)]
    xd = [P.sb("xd%d" % i, [128, 256], BF16) for i in range(2)]
    arow = [P.sb("arow%d" % i, [128, 128]) for i in range(2)]
    Ee = [P.sb("Ee%d" % i, [128, 4, 128]) for i in range(2)]
    MT = [P.sb("MT%d" % i, [128, 4, 128], BF16) for i in range(2)]
    yo, yo_b = P.sb("yo", [128, 4, 64])
    y1, y1_b = P.sb("y1", [128, 256])
    sz, sz_b = P.sb("sz", [128, 256])
    sqj, sqj_b = P.sb("sqj", [128, 256])
    ssq, ssq_b = P.sb("ssq", [128, 1])
    yout = [P.sb("yout%d" % i, [128, 256], BF16) for i in range(2)]
    pTr, pTr_b = P.ps("pTr", [128, 3, 128], BF16)
    pD = pbig
    pCB, pCB_b = P.ps("pCB", [128, 128])
    pY, pY_b = P.ps("pY", [128, 512])
    pS, pS_b = P.ps("pS", [128, 256])
    pRow, pRow_b = P.ps("pRow", [128, 128])

    nbatch = nchunks // 4

    def batch_pro(bi):
        T0 = bi * 512
        xct, xcb = xc[bi % 2]
        if bi == 0:
            P.dma("sp", xct[:, :, 3:515], xbcT[:, 0:512].rearrange("(k p) t -> p k t", p=128), writes=[xcb])
        else:
            P.dma("sp", xct, xbcT[:, T0 - 3:T0 + 512].rearrange("(k p) t -> p k t", p=128), writes=[xcb])
        ztt, ztb = zt[bi % 2]
        P.dma("sp", ztt, z_d[T0:T0 + 512, :].rearrange("(c p) f -> p c f", p=128), writes=[ztb])
        xst, xsb = xsT[bi % 2]
        for k in range(4):
            for j in range(4):
                P.op("pe", lambda e: e.matmul(pcv, lhsT=dgw[:, k, j, :], rhs=xct[:, k, j:j + 512], start=(j == 0), stop=(j == 3)),
                     reads=[dgw_b, xcb], writes=[pcv_b], inc=(j == 3))
            P.op("act", lambda e: e.activation(out=xst[:, k, :], in_=pcv, func=AF.Silu, bias=cb[:, k, :]),
                 reads=[pcv_b, cb_b], writes=[xsb])
        szt, sztb = szb[bi % 2]
        P.op("act", lambda e: e.activation(out=szt, in_=ztt, func=AF.Silu), reads=[ztb], writes=[sztb])

    def front(c):
        bi, cc = c // 4, c % 4
        o = cc * 128
        par = c % 2
        xst, xsb = xsT[bi % 2]
        ztt, ztb = zt[bi % 2]
        xt_, xtb = xtok[par]
        x3 = xt_[:, 0:256].rearrange("p (h d) -> p h d", d=64)
        xdt_, xdtb = xdt[par]
        xw_, xwb = xw[par]
        xd_, xdb = xd[par]
        ar, arb = arow[par]
        pd, pdb = pD[par]
        ee, eeb = Ee[par]
        mt, mtb = MT[par]
        for i in range(3):
            P.op("pe", lambda e: e.transpose(out=pTr[:, i, :], in_=xst[:, i, o:o + 128], identity=ident),
                 reads=[xsb, ident_b], writes=[pTr_b], inc=(i == 2))
        xt_, xtb = xtok[par]
        P.op("act", lambda e: e.copy(out=xt_, in_=pTr.rearrange("p a b -> p (a b)")), reads=[pTr_b], writes=[xtb])
        x3 = xt_[:, 0:256].rearrange("p (h d) -> p h d", d=64)
        xdt_, xdtb = xdt[par]
        xw_, xwb = xw[par]
        xd_, xdb = xd[par]
        P.op("pool", lambda e: e.tensor_tensor(out=xdt_.rearrange("p (h d) -> p h d", d=64), in0=x3,
                                               in1=dts[:, c, :, None].to_broadcast([128, 4, 64]), op=ALU.mult),
             reads=[xtb, dts_b], writes=[xdtb])
        P.op("pool", lambda e: e.tensor_tensor(out=xw_.rearrange("p (h d) -> p h d", d=64), in0=x3,
                                               in1=dtw[:, c, :, None].to_broadcast([128, 4, 64]), op=ALU.mult),
             reads=[xtb, dtw_b], writes=[xwb])
        P.op("pool", lambda e: e.tensor_tensor(out=xd_.rearrange("p (h d) -> p h d", d=64), in0=x3,
                                               in1=dsk[:, :, None].to_broadcast([128, 4, 64]), op=ALU.mult),
             reads=[xtb, dsk_b], writes=[xdb])
        P.op("pe", lambda e: e.matmul(pRow, lhsT=aalp[:, c * 4:c * 4 + 128], rhs=triu, start=True, stop=True),
             reads=[aal_b, triu_b], writes=[pRow_b])
        ar, arb = arow[par]
        P.op("dve", lambda e: e.tensor_copy(out=ar, in_=pRow), reads=[pRow_b], writes=[arb])
        pd, pdb = pD[par]
        for e_ in range(4):
            osl = slice(e_ * 128, (e_ + 1) * 128)
            P.op("pe", lambda e: e.matmul(pd[:, osl], lhsT=sel[:, osl], rhs=ar, start=True, stop=False),
                 reads=[sel_b, arb], writes=[pdb], inc=False)
            P.op("pe", lambda e: e.matmul(pd[:, osl], lhsT=ar, rhs=nsel[:, osl], start=False, stop=False),
                 reads=[nsel_b, arb], writes=[pdb], inc=False)
            P.op("pe", lambda e: e.matmul(pd[:, osl], lhsT=ident, rhs=negm, start=False, stop=True),
                 reads=[ident_b, negm_b], writes=[pdb], inc=(e_ == 3))
        P.op("pe", lambda e: e.matmul(pCB, lhsT=xst[:, 2, o:o + 128], rhs=xst[:, 3, o:o + 128], start=True, stop=True),
             reads=[xsb], writes=[pCB_b])
        ee, eeb = Ee[par]
        P.op("act", lambda e: e.activation(out=ee.rearrange("p a b -> p (a b)"), in_=pd, func=AF.Exp), reads=[pdb], writes=[eeb])
        mt, mtb = MT[par]
        P.op("dve", lambda e: e.tensor_tensor(out=mt, in0=ee, in1=pCB[:, None, :].to_broadcast([128, 4, 128]), op=ALU.mult),
             reads=[eeb, pCB_b], writes=[mtb])

    def back(c):
        bi, cc = c // 4, c % 4
        o = cc * 128
        par = c % 2
        xst, xsb = xsT[bi % 2]
        ztt, ztb = zt[bi % 2]
        xt_, xtb = xtok[par]
        x3 = xt_[:, 0:256].rearrange("p (h d) -> p h d", d=64)
        xdt_, xdtb = xdt[par]
        xw_, xwb = xw[par]
        xd_, xdb = xd[par]
        ar, arb = arow[par]
        pd, pdb = pD[par]
        ee, eeb = Ee[par]
        mt, mtb = MT[par]
        for e_ in range(4):
            fs = slice(e_ * 64, (e_ + 1) * 64)
            P.op("pe", lambda e: e.matmul(pY[:, fs], lhsT=mt[:, e_, :], rhs=xdt_[:, fs], start=True, stop=False),
                 reads=[mtb, xdtb], writes=[pY_b], inc=False)
            P.op("pe", lambda e: e.matmul(pY[:, fs], lhsT=ident, rhs=xd_[:, fs], start=False, stop=True),
                 reads=[ident_b, xdb], writes=[pY_b], inc=False)
        P.op("pe", lambda e: e.matmul(pY[:, 256:512], lhsT=xst[:, 3, o:o + 128], rhs=Sbf, start=True, stop=True),
             reads=[xsb, Sbf_b], writes=[pY_b])
        P.op("pe", lambda e: e.matmul(pS, lhsT=xt_[:, 256:384], rhs=xw_, start=True, stop=True),
             reads=[xtb, xwb], writes=[pS_b])
        P.op("dve", lambda e: e.tensor_tensor(out=S32, in0=S32, in1=cdec[:, c, :, None].to_broadcast([128, 4, 64]), op=ALU.mult),
             reads=[S32_b, cdec_b], writes=[S32_b])
        P.op("dve", lambda e: e.tensor_tensor(out=S32.rearrange("p h d -> p (h d)"), in0=pS, in1=S32.rearrange("p h d -> p (h d)"), op=ALU.add),
             reads=[S32_b, pS_b], writes=[S32_b])
        P.op("act", lambda e: e.copy(out=Sbf, in_=S32.rearrange("p h d -> p (h d)")), reads=[S32_b], writes=[Sbf_b])
        P.op("dve", lambda e: e.tensor_tensor(out=yo, in0=pY[:, 256:512].rearrange("p (h d) -> p h d", d=64),
                                              in1=eac[:, c, :, None].to_broadcast([128, 4, 64]), op=ALU.mult),
             reads=[pY_b, eac_b], writes=[yo_b])
        P.op("dve", lambda e: e.tensor_tensor(out=y1, in0=pY[:, 0:256], in1=yo.rearrange("p h d -> p (h d)"), op=ALU.add),
             reads=[pY_b, yo_b], writes=[y1_b])
        szt, sztb = szb[bi % 2]
        P.op("pool", lambda e: e.tensor_tensor(out=y1, in0=y1, in1=szt[:, cc, :], op=ALU.mult), reads=[y1_b, sztb], writes=[y1_b])
        P.op("dve", lambda e: e.scalar_tensor_tensor(out=sqj, in0=y1, scalar=1.0 / 256, in1=y1, op0=ALU.mult, op1=ALU.mult, accum_out=ssq),
             reads=[y1_b], writes=[sqj_b, ssq_b])
        P.op("act", lambda e: e.activation(out=ssq, in_=ssq, func=AF.Ln, bias=1e-6), reads=[ssq_b], writes=[ssq_b])
        P.op("act", lambda e: e.activation(out=ssq, in_=ssq, func=AF.Exp, scale=-0.5), reads=[ssq_b], writes=[ssq_b])
        yt_, ytb = yout[par]
        P.op("dve", lambda e: e.scalar_tensor_tensor(out=yt_, in0=y1, scalar=ssq, in1=gn, op0=ALU.mult, op1=ALU.mult),
             reads=[y1_b, ssq_b, gn_b], writes=[ytb])
        P.dma("sp", y_o[c * 128:(c + 1) * 128, :], yt_, reads=[ytb])

    batch_pro(0)
    front(0)
    for c in range(nchunks):
        if c + 1 < nchunks:
            if (c + 1) % 4 == 0:
                batch_pro((c + 1) // 4)
            front(c + 1)
        back(c)
    P.finish()
    return nc
def build_ret(nchunks=NCH):
    nc = bass.Bass("TRN2", target_bir_lowering=False)
    P = Prog(nc)
    Lc = nchunks * 128

    def din(name, shape, dt=F32):
        return nc.dram_tensor(name, list(shape), dt, kind="ExternalInput").ap()

    qT_d = din("qT", [256, Lc], BF16)
    kT_d = din("kT", [256, Lc], BF16)
    v_d = din("v", [Lc, 256], BF16)
    lg_d = din("lg", [128, 1])
    dlt_d = din("dlt", [128, 128])
    rowl_d = din("rowl", [128, 128])
    colk_d = din("colk", [128, 1])
    ident_d = din("ident", [128, 128], BF16)
    o_o = nc.dram_tensor("o", [Lc, 256], BF16, kind="ExternalOutput").ap()

    def ld(name, shape, src, dt=F32):
        t, b = P.sb(name, shape, dt)
        P.dma("sp", t, src, writes=[b])
        return t, b

    ident, ident_b = ld("ident_s", [128, 128], ident_d, BF16)
    lg, lg_b = ld("lg_s", [128, 1], lg_d)
    dlt, dlt_b = ld("dlt_s", [128, 128], dlt_d)
    rowl, rowl_b = ld("rowl_s", [128, 128], rowl_d)
    colk, colk_b = ld("colk_s", [128, 1], colk_d)
    intraT, intraT_b = P.sb("intraT", [128, 128])
    gq, gq_b = P.sb("gq", [128, 128], BF16)
    kd, kd_b = P.sb("kdcol", [128, 1])
    cdec, cdec_b = P.sb("cdecr", [128, 1])
    P.op("act", lambda e: e.activation(out=intraT, in_=dlt, func=AF.Exp, scale=lg), reads=[dlt_b, lg_b], writes=[intraT_b])
    P.op("act", lambda e: e.activation(out=gq, in_=rowl, func=AF.Exp, scale=lg), reads=[rowl_b, lg_b], writes=[gq_b])
    P.op("act", lambda e: e.activation(out=kd, in_=colk, func=AF.Exp, scale=lg), reads=[colk_b, lg_b], writes=[kd_b])
    P.op("dve", lambda e: e.tensor_scalar(out=cdec, in0=lg, scalar1=128.0, scalar2=None, op0=ALU.mult), reads=[lg_b], writes=[cdec_b])
    P.op("act", lambda e: e.activation(out=cdec, in_=cdec, func=AF.Exp), reads=[cdec_b], writes=[cdec_b])

    qTb = [P.sb("qTb%d" % i, [128, 2, 512], BF16) for i in range(2)]
    kTb = [P.sb("kTb%d" % i, [128, 2, 512], BF16) for i in range(2)]
    vb = [P.sb("vb%d" % i, [128, 4, 256], BF16) for i in range(2)]
    S32, S32_b = P.sb("S32", [128, 512])
    Sbf, Sbf_b = P.sb("Sbf", [128, 2, 256], BF16)
    P.op("pool", lambda e: e.memset(S32, 0.0), writes=[S32_b])
    P.op("pool", lambda e: e.memset(Sbf, 0.0), writes=[Sbf_b])
    PT = [P.sb("PT%d" % i, [128, 128], BF16) for i in range(2)]
    kdec = [P.sb("kdec%d" % i, [128, 256], BF16) for i in range(2)]
    qdec = [P.sb("qdec%d" % i, [128, 2, 128], BF16) for i in range(2)]
    ot = [P.sb("ot%d" % i, [128, 256], BF16) for i in range(2)]
    pSc = [P.ps("pSc%d" % i, [128, 128]) for i in range(2)]
    pTr = [P.ps("pTr%d" % i, [128, 2, 128], BF16) for i in range(2)]
    pO = [P.ps("pOr%d" % i, [128, 256]) for i in range(2)]
    pS, pS_b = P.ps("pSr", [128, 512])

    for bi in range(nchunks // 4):
        T0 = bi * 512
        qt, qtb = qTb[bi % 2]
        kt, ktb = kTb[bi % 2]
        vt, vtb = vb[bi % 2]
        P.dma("sp", qt, qT_d[:, T0:T0 + 512].rearrange("(k p) t -> p k t", p=128), writes=[qtb])
        P.dma("sp", kt, kT_d[:, T0:T0 + 512].rearrange("(k p) t -> p k t", p=128), writes=[ktb])
        P.dma("sp", vt, v_d[T0:T0 + 512, :].rearrange("(c p) f -> p c f", p=128), writes=[vtb])
        for cc in range(4):
            c = bi * 4 + cc
            o = cc * 128
            par = c % 2
            psc, pscb = pSc[par]
            for k in range(2):
                P.op("pe", lambda e: e.matmul(psc, lhsT=kt[:, k, o:o + 128], rhs=qt[:, k, o:o + 128], start=(k == 0), stop=(k == 1)),
                     reads=[ktb, qtb], writes=[pscb], inc=(k == 1))
            ptr, ptrb = pTr[par]
            for k in range(2):
                P.op("pe", lambda e: e.transpose(out=ptr[:, k, :], in_=kt[:, k, o:o + 128], identity=ident),
                     reads=[ktb, ident_b], writes=[ptrb], inc=(k == 1))
            pt_, ptb_ = PT[par]
            P.op("dve", lambda e: e.tensor_tensor(out=pt_, in0=psc, in1=intraT, op=ALU.mult), reads=[pscb, intraT_b], writes=[ptb_])
            kdc, kdcb = kdec[par]
            P.op("act", lambda e: e.activation(out=kdc, in_=ptr.rearrange("p a b -> p (a b)"), func=AF.Copy, scale=kd),
                 reads=[ptrb, kd_b], writes=[kdcb])
            qd, qdb = qdec[par]
            P.op("pool", lambda e: e.tensor_tensor(out=qd, in0=qt[:, :, o:o + 128], in1=gq[:, None, :].to_broadcast([128, 2, 128]), op=ALU.mult),
                 reads=[qtb, gq_b], writes=[qdb])
            po, pob = pO[par]
            P.op("pe", lambda e: e.matmul(po, lhsT=pt_, rhs=vt[:, cc, :], start=True, stop=False), reads=[ptb_, vtb], writes=[pob], inc=False)
            for k in range(2):
                P.op("pe", lambda e: e.matmul(po, lhsT=qd[:, k, :], rhs=Sbf[:, k, :], start=False, stop=(k == 1)),
                     reads=[qdb, Sbf_b], writes=[pob], inc=(k == 1))
            for k in range(2):
                P.op("pe", lambda e: e.matmul(pS[:, k * 256:(k + 1) * 256], lhsT=kdc[:, k * 128:(k + 1) * 128], rhs=vt[:, cc, :], start=True, stop=True),
                     reads=[kdcb, vtb], writes=[pS_b], inc=(k == 1))
            P.op("dve", lambda e: e.scalar_tensor_tensor(out=S32, in0=S32, scalar=cdec, in1=pS, op0=ALU.mult, op1=ALU.add),
                 reads=[S32_b, cdec_b, pS_b], writes=[S32_b])
            P.op("act", lambda e: e.copy(out=Sbf.rearrange("p a b -> p (a b)"), in_=S32), reads=[S32_b], writes=[Sbf_b])
            ott, otb = ot[par]
            P.op("dve", lambda e: e.tensor_copy(out=ott, in_=po), reads=[pob], writes=[otb])
            P.dma("sp", o_o[c * 128:(c + 1) * 128, :], ott, reads=[otb])
    P.finish()
    return nc
TOPK = 256
NIT = 20
BIG = 1.0e30


def build_dsa(nblk=16, Lk=L_SEQ, stride=8):
    nc = bass.Bass("TRN2", target_bir_lowering=False)
    P = Prog(nc)

    def din(name, shape, dt=F32):
        return nc.dram_tensor(name, list(shape), dt, kind="ExternalInput").ap()

    q_d = din("q_l", [nblk, 128, 2048], BF16)
    qi_d = din("qi_l", [nblk, 128, 2048], BF16)
    wi_d = din("wi", [nblk * 128, 16])
    kT_d = din("kT", [256, Lk], BF16)
    kiT_d = din("kiT", [64, Lk], BF16)
    v_d = din("v", [Lk, 256], BF16)
    qpos_d = din("qpos", [128, nblk])
    kidx_d = din("kidx", [128, 1024])
    ident_d = din("ident", [128, 128], BF16)
    identf_d = din("identf", [128, 128])
    o_o = nc.dram_tensor("o", [nblk * 128, 1024], BF16, kind="ExternalOutput").ap()

    def ld(name, shape, src, dt=F32, q="sp"):
        t, b = P.sb(name, shape, dt)
        P.dma(q, t, src, writes=[b])
        return t, b

    ident, ident_b = ld("ident_s", [128, 128], ident_d, BF16)
    identf, identf_b = ld("identf_s", [128, 128], identf_d)
    oT, oT_b = P.sb("oT", [128, 512])
    kidx, kidx_b = ld("kidx_s", [128, 1024], kidx_d)
    qpos, qpos_b = ld("qpos_s", [128, nblk], qpos_d)
    kiTs, kiTs_b = P.sb("kiTs", [128, Lk], BF16)
    P.op("pool", lambda e: e.memset(kiTs[64:128, :], 0.0), writes=[kiTs_b])
    for c0 in range(0, Lk, 4096):
        c1 = min(Lk, c0 + 4096)
        P.dma("sp", kiTs[0:64, c0:c1], kiT_d[:, c0:c1], writes=[kiTs_b])
    kTb = [P.sb("kTb%d" % i, [128, 2, 128], BF16) for i in range(4)]
    scs = [P.sb("scores%d" % i, [128, Lk]) for i in range(2)]
    qg = [P.sb("qg%d" % i, [128, 4, 4, 128], BF16) for i in range(1)]
    qig = [P.sb("qig%d" % i, [128, 16, 128], BF16) for i in range(1)]
    wit, wit_b = P.sb("wit", [128, 16])
    wab, wab_b = P.sb("wab", [128, 16])
    wsg, wsg_b = P.sb("wsg", [128, 16])
    dg, dg_b = P.sb("dg", [128, 16, 128], BF16)
    rt = [P.sb("rt%d" % i, [128, 512], BF16) for i in range(3)]
    pen, pen_b = P.sb("pen", [128, 1024])
    tmpw, tmpw_b = P.sb("tmpw", [128, 1024])
    JCH = 4096
    junk, junk_b = P.sb("junk", [128, JCH], U8)
    cols = {}
    for nm in ("thc", "rmin", "rmin2", "rmax", "lo", "w", "mid", "cnt", "cnt2", "ge"):
        cols[nm] = P.sb("c_" + nm, [128, 1])
    vb = []
    for i in range(4):
        t, b = P.sb("vb%d" % i, [128, 4, 65], BF16)
        P.op("pool", lambda e: e.memset(t, 1.0), writes=[b])
        vb.append((t, b))
    selb = [P.sb("selb%d" % i, [128, 128], BF16) for i in range(2)]
    Et = [P.sb("Et%d" % i, [128, 4, 128], BF16) for i in range(3)]
    PTt = [P.sb("PTt%d" % i, [128, 4, 128], BF16) for i in range(3)]
    rden, rden_b = P.sb("rden", [128, 16])
    osb, osb_b = P.sb("osb", [128, 16, 64], BF16)
    pW = [P.ps("pW%d" % i, [128, 512]) for i in range(2)]
    pSc, pSc_b = P.ps("pScd", [128, 512])
    pSel = [P.ps("pSel0", [128, 128], BF16)]
    pOT = [P.ps("pOT%d" % i, [128, 512]) for i in range(4)]


    nW = [0]
    nR = [0]
    nV = [0]
    nE = [0]
    def phaseA(i):
        nk = stride * (i + 1)
        S = nk * 128
        sc, sc_b = scs[i % 2]
        qit, qib = qig[0]
        P.dma("sp", qit.rearrange("p a b -> p (a b)"), qi_d[i], writes=[qib])
        P.dma("sp", wit, wi_d[i * 128:(i + 1) * 128, :], writes=[wit_b])
        P.op("act", lambda e: e.activation(out=wab, in_=wit, func=AF.Abs), reads=[wit_b], writes=[wab_b])
        P.op("dve", lambda e: e.tensor_scalar(out=wsg, in0=wit, scalar1=0.0, scalar2=2.0, op0=ALU.is_gt, op1=ALU.mult),
             reads=[wit_b], writes=[wsg_b])
        P.op("dve", lambda e: e.tensor_scalar(out=wsg, in0=wsg, scalar1=-1.0, scalar2=None, op0=ALU.add), reads=[wsg_b], writes=[wsg_b])
        for h in range(16):
            P.op("pool", lambda e: e.tensor_scalar(out=dg[:, h, :], in0=ident, scalar1=wsg[:, h:h + 1], scalar2=None, op0=ALU.mult),
                 reads=[ident_b, wsg_b], writes=[dg_b])
        steps = [(kt, h) for kt in range(nk // 4) for h in range(16)]

        def back(info):
            kt, h, pw, pwb = info
            ks = slice(kt * 512, (kt + 1) * 512)
            r, rb = rt[nR[0] % 3]
            nR[0] += 1
            P.op("act", lambda e: e.activation(out=r, in_=pw, func=AF.Relu, scale=wab[:, h:h + 1]),
                 reads=[pwb, wab_b], writes=[rb])
            P.op("pe", lambda e: e.matmul(pSc, lhsT=dg[:, h, :], rhs=r, start=(h == 0), stop=(h == 15)),
                 reads=[dg_b, rb], writes=[pSc_b], inc=(h == 15))
            if h == 15:
                P.op("act", lambda e: e.copy(out=sc[:, ks], in_=pSc), reads=[pSc_b], writes=[sc_b])

        prev = None
        for kt, h in steps:
            ks = slice(kt * 512, (kt + 1) * 512)
            pw, pwb = pW[nW[0] % 2]
            nW[0] += 1
            P.op("pe", lambda e: e.matmul(pw, lhsT=qit[:, h, :], rhs=kiTs[:, ks], start=True, stop=True),
                 reads=[qib, kiTs_b], writes=[pwb])
            if prev is not None:
                back(prev)
            prev = (kt, h, pw, pwb)
        back(prev)
    def phaseB(i):
        nk = stride * (i + 1)
        S = nk * 128
        sc, sc_b = scs[i % 2]
        base = S - 1024
        thc, thc_b = cols["thc"]
        P.op("dve", lambda e: e.tensor_scalar(out=thc, in0=qpos[:, i:i + 1], scalar1=float(-base), scalar2=None, op0=ALU.add),
             reads=[qpos_b], writes=[thc_b])
        P.op("dve", lambda e: e.tensor_scalar(out=pen, in0=kidx, scalar1=thc, scalar2=-BIG, op0=ALU.is_gt, op1=ALU.mult),
             reads=[kidx_b, thc_b], writes=[pen_b])
        win = sc[:, base:S]
        P.op("dve", lambda e: e.tensor_tensor(out=tmpw, in0=win, in1=pen, op=ALU.subtract), reads=[sc_b, pen_b], writes=[tmpw_b])
        rmin, rmin_b = cols["rmin"]
        rmin2, rmin2_b = cols["rmin2"]
        rmax, rmax_b = cols["rmax"]
        P.op("dve", lambda e: e.tensor_reduce(out=rmin, in_=tmpw, op=ALU.min, axis=AX.X), reads=[tmpw_b], writes=[rmin_b])
        if base > 0:
            P.op("dve", lambda e: e.tensor_reduce(out=rmin2, in_=sc[:, 0:base], op=ALU.min, axis=AX.X), reads=[sc_b], writes=[rmin2_b])
            P.op("dve", lambda e: e.tensor_tensor(out=rmin, in0=rmin, in1=rmin2, op=ALU.min), reads=[rmin_b, rmin2_b], writes=[rmin_b])
        P.op("dve", lambda e: e.tensor_tensor(out=win, in0=win, in1=pen, op=ALU.add), reads=[sc_b, pen_b], writes=[sc_b])
        P.op("dve", lambda e: e.tensor_reduce(out=rmax, in_=sc[:, 0:S], op=ALU.max, axis=AX.X), reads=[sc_b], writes=[rmax_b])
        lo, lo_b = cols["lo"]
        w, w_b = cols["w"]
        mid, mid_b = cols["mid"]
        cnt, cnt_b = cols["cnt"]
        ge, ge_b = cols["ge"]
        P.op("dve", lambda e: e.tensor_copy(out=lo, in_=rmin), reads=[rmin_b], writes=[lo_b])
        P.op("dve", lambda e: e.tensor_tensor(out=w, in0=rmax, in1=rmin, op=ALU.subtract), reads=[rmax_b, rmin_b], writes=[w_b])
        for it in range(NIT):
            P.op("dve", lambda e: e.tensor_scalar(out=w, in0=w, scalar1=0.5, scalar2=None, op0=ALU.mult), reads=[w_b], writes=[w_b])
            P.op("dve", lambda e: e.tensor_tensor(out=mid, in0=lo, in1=w, op=ALU.add), reads=[lo_b, w_b], writes=[mid_b])
            cA, cA_b = cols["cnt"]
            cB, cB_b = cols["cnt2"]
            for j0 in range(0, S, JCH):
                j1 = min(S, j0 + JCH)
                if j0 == 0:
                    P.op("dve", lambda e: e.tensor_scalar(out=junk[:, 0:j1 - j0], in0=sc[:, j0:j1], scalar1=mid, scalar2=None,
                                                          op0=ALU.is_ge, op1=ALU.add, accum_out=cA),
                         reads=[sc_b, mid_b], writes=[junk_b, cA_b])
                else:
                    P.op("dve", lambda e: e.tensor_scalar(out=junk[:, 0:j1 - j0], in0=sc[:, j0:j1], scalar1=mid, scalar2=cB,
                                                          op0=ALU.is_ge, op1=ALU.add, accum_out=cA),
                         reads=[sc_b, mid_b, cB_b], writes=[junk_b, cA_b])
                cA, cA_b, cB, cB_b = cB, cB_b, cA, cA_b
            cnt, cnt_b = cB, cB_b
            P.op("dve", lambda e: e.tensor_scalar(out=ge, in0=cnt, scalar1=TOPK - 0.5, scalar2=None, op0=ALU.is_ge),
                 reads=[cnt_b], writes=[ge_b])
            P.op("dve", lambda e: e.scalar_tensor_tensor(out=lo, in0=w, scalar=ge, in1=lo, op0=ALU.mult, op1=ALU.add),
                 reads=[w_b, ge_b, lo_b], writes=[lo_b])
    def phaseD(i):
        nk = stride * (i + 1)
        S = nk * 128
        sc, sc_b = scs[i % 2]
        qgt, qgb = qg[0]
        lo, lo_b = cols['lo']
        P.dma("sp", qgt.rearrange("p a b c -> p (a b c)"), q_d[i], writes=[qgb])
        def backD(info):
            kb, g, pw, pwb, vt, vtb, psl, pslb = info
            et, etb = Et[nE[0] % 3]
            ptt, pttb = PTt[nE[0] % 3]
            nE[0] += 1
            P.op("act", lambda e: e.activation(out=et.rearrange("p a b -> p (a b)"), in_=pw, func=AF.Exp, scale=0.125),
                 reads=[pwb], writes=[etb])
            P.op("dve", lambda e: e.tensor_tensor(out=ptt, in0=et, in1=psl[:, None, :].to_broadcast([128, 4, 128]), op=ALU.mult),
                 reads=[etb, pslb], writes=[pttb])
            po, pob = pOT[g]
            P.op("pe", lambda e: e.matmul(po[0:65, :], lhsT=vt[:, g, :], rhs=ptt.rearrange("p a b -> p (a b)"),
                                          start=(kb == 0), stop=(kb == nk - 1)),
                 reads=[pttb, vtb], writes=[pob])

        prev = None
        for kb in range(nk):
            vt, vtb = vb[nV[0] % 4]
            ktt, kttb = kTb[nV[0] % 4]
            nV[0] += 1
            P.dma("sp", vt[:, :, 0:64], v_d[kb * 128:(kb + 1) * 128, :].rearrange("p (g d) -> p g d", d=64), writes=[vtb])
            P.dma("sp", ktt, kT_d[:, kb * 128:(kb + 1) * 128].rearrange("(k p) t -> p k t", p=128), writes=[kttb])
            for g in range(4):
                gl, gh = g % 2, g // 2
                pw, pwb = pW[nW[0] % 2]
                nW[0] += 1
                P.op("pe", lambda e: e.matmul(pw, lhsT=ktt[:, gh, :], rhs=qgt[:, g, :, :], start=True, stop=True),
                     reads=[kttb, qgb], writes=[pwb])
                if prev is not None:
                    backD(prev)
                if g == 0:
                    sb_, sbb = selb[kb % 2]
                    P.op("dve", lambda e: e.tensor_scalar(out=sb_, in0=sc[:, kb * 128:(kb + 1) * 128], scalar1=lo, scalar2=None, op0=ALU.is_ge),
                         reads=[sc_b, lo_b], writes=[sbb])
                    psl, pslb = pSel[0]
                    P.op("pe", lambda e: e.transpose(out=psl, in_=sb_, identity=ident), reads=[sbb, ident_b], writes=[pslb])
                prev = (kb, g, pw, pwb, vt, vtb, psl, pslb)
        backD(prev)
        for g in range(4):
            po, pob = pOT[g]
            P.op("act", lambda e: e.copy(out=oT[0:65, :], in_=po[0:65, :]), reads=[pob], writes=[oT_b])
            pw, pwb = pW[nW[0] % 2]
            nW[0] += 1
            for a in range(4):
                P.op("pe", lambda e: e.matmul(pw[:, a * 65:(a + 1) * 65], lhsT=oT[0:65, a * 128:(a + 1) * 128], rhs=identf[0:65, 0:65],
                                              start=True, stop=True),
                     reads=[oT_b, identf_b], writes=[pwb], inc=(a == 3))
            p3 = pw[:, 0:260].rearrange("p (h d) -> p h d", d=65)
            P.op("dve", lambda e: e.reciprocal(out=rden[:, g * 4:g * 4 + 4], in_=p3[:, :, 64]), reads=[pwb], writes=[rden_b])
            P.op("dve", lambda e: e.tensor_tensor(out=osb[:, g * 4:g * 4 + 4, :], in0=p3[:, :, 0:64],
                                                  in1=rden[:, g * 4:g * 4 + 4, None].to_broadcast([128, 4, 64]), op=ALU.mult),
                 reads=[pwb, rden_b], writes=[osb_b])
        P.dma("sp", o_o[i * 128:(i + 1) * 128, :], osb.rearrange("p h d -> p (h d)"), reads=[osb_b])

    phaseA(0)
    for i in range(nblk):
        if i + 1 < nblk:
            phaseA(i + 1)
        phaseB(i)
        phaseD(i)
    P.finish()
    return nc
import ml_dtypes as _mld

_BF = _mld.bfloat16
_PROGS = {}


def _prog(key, fn):
    if key not in _PROGS:
        _PROGS[key] = fn()
    return _PROGS[key]


def _run(nc, maps):
    res = run_bass_kernel_spmd(nc, maps, core_ids=list(range(8)))
    return res.results


def _c(a):
    return np.ascontiguousarray(a)


def _ssd_consts():
    k = np.arange(128)
    triu = (k[:, None] <= k[None, :]).astype(np.float32)
    negm = np.where(k[None, :] < k[:, None], -30000.0, 0.0).astype(_BF)
    sel = np.zeros((128, 4, 128), np.float32)
    for e in range(4):
        sel[e, e, :] = 1.0
    return dict(ident=np.eye(128, dtype=_BF), triu=triu, ones=np.ones((128, 128), np.float32), negm=negm,
                sel=sel.reshape(128, 512), nsel=(-sel).reshape(128, 512))


def _ssd_layer(z, xbcT, dt, conv_w, conv_b, dt_bias, a_log, d_skip, gnorm):
    C = _ssd_consts()
    maps = []
    for g in range(8):
        ch = np.concatenate([np.arange(g * 256, (g + 1) * 256), 2048 + np.arange(g * 128, (g + 1) * 128),
                             3072 + np.arange(g * 128, (g + 1) * 128)])
        m = dict(C)
        m.update(xbcT=_c(xbcT[ch]), z=_c(z[:, g * 256:(g + 1) * 256]),
                 dt=_c(dt[:, g * 4:(g + 1) * 4].reshape(NCH, 128, 4).transpose(1, 0, 2).reshape(128, NCH * 4)),
                 cw=_c(conv_w[:, ch].T.reshape(4, 128, 4).transpose(1, 0, 2).reshape(128, 16)),
                 cb=_c(conv_b[ch].reshape(4, 128).T),
                 dtb=_c(dt_bias[None, g * 4:(g + 1) * 4]), alog=_c(a_log[None, g * 4:(g + 1) * 4]),
                 dsk=_c(d_skip[None, g * 4:(g + 1) * 4]), gn=_c(gnorm[None, g * 256:(g + 1) * 256]))
        maps.append(m)
    res = _run(_prog("ssd", build_ssd), maps)
    return np.concatenate([r["y"] for r in res], axis=1)


def kernel(x, positions,
           l0_norm, l0_w_in, l0_conv_w, l0_conv_b, l0_dt_bias, l0_a_log, l0_d_skip, l0_gnorm, l0_w_out,
           l1_norm, l1_w_in, l1_gnorm, l1_w_out,
           l2_norm, l2_w_in, l2_idx_knorm, l2_w_out,
           l3_norm, l3_w_in, l3_conv_w, l3_conv_b, l3_dt_bias, l3_a_log, l3_d_skip, l3_gnorm, l3_w_out,
           final_norm):
    f32 = np.float32
    x = np.asarray(x, f32)[0]
    pos = np.asarray(positions)[0].astype(np.int32)
    L = x.shape[0]
    ident = np.eye(128, dtype=_BF)
    cont = [np.arange(c * NT, (c + 1) * NT) for c in range(8)]
    inter = [np.concatenate([np.arange((8 * i + c) * 128, (8 * i + c + 1) * 128) for i in range(16)]) for c in range(8)]

    maps = [dict(h_in=_c(x[cont[c]]), ident=ident, norm_g=_c(np.asarray(l0_norm, f32)[None, :]), w_in=_c(np.asarray(l0_w_in, f32)))
            for c in range(8)]
    r = _run(_prog("tok_none_ssd", lambda: build_tok(None, "ssd")), maps)
    z = np.concatenate([q["z"] for q in r], 0)
    xbcT = np.concatenate([q["xbcT"] for q in r], 1)
    dt = np.concatenate([q["dt"] for q in r], 0)
    y0 = _ssd_layer(z, xbcT, dt, np.asarray(l0_conv_w, f32), np.asarray(l0_conv_b, f32), np.asarray(l0_dt_bias, f32),
                    np.asarray(l0_a_log, f32), np.asarray(l0_d_skip, f32), np.asarray(l0_gnorm, f32))
    inv_ret = (1.0 / (np.float32(10000.0) ** np.linspace(0.0, 1.0, 128, dtype=np.float32))).astype(f32)[:, None]
    maps = [dict(h_in=_c(x[cont[c]]), a_in=_c(y0[cont[c]]), w_out=_c(np.asarray(l0_w_out, f32)), ident=ident,
                 norm_g=_c(np.asarray(l1_norm, f32)[None, :]), w_in=_c(np.asarray(l1_w_in, f32)),
                 pos=_c(pos[cont[c]][None, :]), inv=_c(inv_ret)) for c in range(8)]
    r = _run(_prog("tok_ssd_ret", lambda: build_tok("ssd", "ret")), maps)
    h1 = np.concatenate([q["h_out"] for q in r], 0)
    qT = np.concatenate([q["qT"] for q in r], 1)
    kT = np.concatenate([q["kT"] for q in r], 1)
    v1 = np.concatenate([q["v"] for q in r], 0)
    sg1 = np.concatenate([q["sg"] for q in r], 0)
    log_g = np.log1p(-np.exp2(-5.0 - np.arange(4, dtype=f32))).astype(f32)
    ii = np.arange(128)
    dlt = np.where(ii[None, :] >= ii[:, None], (ii[None, :] - ii[:, None]).astype(f32), 1e6).astype(f32)
    rowl = _c(np.broadcast_to((ii + 1).astype(f32)[None, :], (128, 128)))
    colk = _c((127 - ii).astype(f32)[:, None])
    maps = []
    for c in range(8):
        hh, half = c // 2, c % 2
        maps.append(dict(qT=_c(qT[hh * 256:(hh + 1) * 256]), kT=_c(kT[hh * 256:(hh + 1) * 256]),
                         v=_c(v1[:, hh * 512 + half * 256: hh * 512 + (half + 1) * 256]),
                         lg=np.full((128, 1), log_g[hh], f32), dlt=dlt, rowl=rowl, colk=colk, ident=ident))
    r = _run(_prog("ret", build_ret), maps)
    o1 = np.concatenate([q["o"] for q in r], 1)
    inv_dsa = (np.float32(500000.0) ** (-np.arange(0, 16, 2, dtype=f32) / 16)).astype(f32)[None, :]
    maps = [dict(h_in=_c(h1[inter[c]]), a_in=_c(o1[inter[c]]), sg_in=_c(sg1[inter[c]]), gn=_c(np.asarray(l1_gnorm, f32)[None, :]),
                 w_out=_c(np.asarray(l1_w_out, f32)), ident=ident, norm_g=_c(np.asarray(l2_norm, f32)[None, :]),
                 w_in=_c(np.asarray(l2_w_in, f32)), pos=_c(pos[inter[c]][:, None]), inv=_c(inv_dsa),
                 knorm=_c(np.asarray(l2_idx_knorm, f32)[None, :])) for c in range(8)]
    r2 = _run(_prog("tok_ret_dsa", lambda: build_tok("ret", "dsa")), maps)
    kT2 = np.zeros((256, L), _BF)
    kiT2 = np.zeros((64, L), _BF)
    v2 = np.zeros((L, 256), _BF)
    for c in range(8):
        kT2[:, inter[c]] = r2[c]["kT"]
        kiT2[:, inter[c]] = r2[c]["kiT"]
        v2[inter[c]] = r2[c]["v"]
    kidx = _c(np.broadcast_to(np.arange(1024, dtype=f32)[None, :], (128, 1024)))
    maps = []
    for c in range(8):
        qTc = r2[c]["qT"].reshape(4, 4, 64, 16, 128)
        q_l = np.zeros((16, 128, 4, 4, 128), _BF)
        for g in range(4):
            gl = g % 2
            q_l[:, gl * 64:(gl + 1) * 64, g] = qTc[g].transpose(2, 1, 0, 3)
        qiTc = r2[c]["qiT"].reshape(16, 64, 16, 128)
        qi_l = np.zeros((16, 128, 16, 128), _BF)
        qi_l[:, 0:64] = qiTc.transpose(2, 1, 0, 3)
        qpos = _c(inter[c].reshape(16, 128).T.astype(f32))
        maps.append(dict(q_l=q_l.reshape(16, 128, 2048), qi_l=qi_l.reshape(16, 128, 2048), wi=_c(r2[c]["wi"]),
                         kT=kT2, kiT=kiT2, v=v2, qpos=qpos, kidx=kidx, ident=ident, identf=np.eye(128, dtype=f32)))
    rd = _run(_prog("dsa", build_dsa), maps)
    maps = [dict(h_in=_c(r2[c]["h_out"]), a_in=_c(rd[c]["o"]), sg_in=_c(r2[c]["sg"]), w_out=_c(np.asarray(l2_w_out, f32)),
                 ident=ident, norm_g=_c(np.asarray(l3_norm, f32)[None, :]), w_in=_c(np.asarray(l3_w_in, f32))) for c in range(8)]
    r3 = _run(_prog("tok_dsa_ssd", lambda: build_tok("dsa", "ssd")), maps)
    z = np.zeros((L, 2048), _BF)
    xbcT = np.zeros((4096, L), _BF)
    dt = np.zeros((L, 32), f32)
    for c in range(8):
        z[inter[c]] = r3[c]["z"]
        xbcT[:, inter[c]] = r3[c]["xbcT"]
        dt[inter[c]] = r3[c]["dt"]
    y3 = _ssd_layer(z, xbcT, dt, np.asarray(l3_conv_w, f32), np.asarray(l3_conv_b, f32), np.asarray(l3_dt_bias, f32),
                    np.asarray(l3_a_log, f32), np.asarray(l3_d_skip, f32), np.asarray(l3_gnorm, f32))
    maps = [dict(h_in=_c(r3[c]["h_out"]), a_in=_c(y3[inter[c]]), w_out=_c(np.asarray(l3_w_out, f32)), ident=ident,
                 norm_g=_c(np.asarray(final_norm, f32)[None, :])) for c in range(8)]
    rf = _run(_prog("tok_ssd_final", lambda: build_tok("ssd", "final")), maps)
    out = np.zeros((1, L, D), f32)
    for c in range(8):
        out[0, inter[c]] = rf[c]["out"]
    global _DBG
    _DBG = dict(h1=h1, r2=r2, r3=r3, inter=inter)
    return out
```
